# Optimizing a Trainium2 kernel written in Bass

```python
import math
import jax, jax.numpy as jnp
from jax import lax
import numpy as np

D_MODEL = 1024
BATCH = 2
SEQ = 8192
DEPTH = 1
DEC_BATCH = 128
DEC_SEQ = 8
PAST_LEN = 8192
PAGE_SIZE = 128

GLA_HEADS = 4
GLA_DK = D_MODEL // 2 // GLA_HEADS
GLA_DV = D_MODEL // GLA_HEADS
GLA_QK = GLA_HEADS * GLA_DK
GLA_V = GLA_HEADS * GLA_DV
GLA_RANK = 16
GLA_NORMALIZER = 16.0
GLA_CHUNK = 16
DIL_GROUPS = ((128, 1), (512, 4), (2048, 16))
N_GROUPS = len(DIL_GROUPS)
DIL_HEADS = 8
DIL_HD = 64
DIL_WIDTH = DIL_HEADS * DIL_HD
DIL_QKV = N_GROUPS * DIL_WIDTH
Q_BLOCK = 128
ROPE_THETA = 10000.0
D_FF = (8 * D_MODEL // 3 + 127) // 128 * 128
CONV_W = 3
PLE_DIM = 256
ALPHA = (2.0 * DEPTH) ** 0.25
BETA = (8.0 * DEPTH) ** -0.25
NORM_EPS = 1e-5
IN_SPLITS = (GLA_QK, GLA_QK, GLA_V, GLA_V, GLA_RANK, DIL_QKV, DIL_QKV, DIL_QKV, D_MODEL, D_MODEL)
IN_COLS = sum(IN_SPLITS)
SPLIT_AT = tuple(int(s) for s in np.cumsum(IN_SPLITS)[:-1])

kernel_name = 'hybrid_gla_dilated_convffn_decoder'


def layer_norm(x, g, b):
    xf = x.astype(jnp.float32)
    mu = jnp.mean(xf, -1, keepdims=True)
    var = jnp.mean(jnp.square(xf - mu), -1, keepdims=True)
    y = (xf - mu) * lax.rsqrt(var + NORM_EPS) * g.astype(jnp.float32) + b.astype(jnp.float32)
    return y.astype(x.dtype)


def rope(x, pos):
    half = x.shape[-1] // 2
    inv = ROPE_THETA ** (-jnp.arange(half, dtype=jnp.float32) / half)
    ang = pos.astype(jnp.float32)[:, None] * inv[None, :]
    cos = jnp.cos(ang)[None, :, None, :]
    sin = jnp.sin(ang)[None, :, None, :]
    xf = x.astype(jnp.float32)
    x1, x2 = xf[..., :half], xf[..., half:]
    return jnp.concatenate([x1 * cos - x2 * sin, x2 * cos + x1 * sin], -1).astype(x.dtype)


def gla_recurrence(q, k, v, logd, s0):
    B, T, H, _ = q.shape
    DV = v.shape[-1]
    C = GLA_CHUNK
    pad = (-T) % C
    n = (T + pad) // C

    def blocks(a):
        a = jnp.pad(a.astype(jnp.float32), ((0, 0), (0, pad), (0, 0), (0, 0)))
        return a.reshape(B, n, C, H, a.shape[-1]).transpose(1, 0, 3, 2, 4)

    qc, kc, vc, gc = blocks(q), blocks(k), blocks(v), blocks(logd)
    causal = jnp.tril(jnp.ones((C, C), bool))

    def step(S, inp):
        qb, kb, vb, gb = inp
        b = jnp.cumsum(gb, axis=2)
        rel = jnp.where(causal[None, None, :, :, None],
                        b[:, :, :, None, :] - b[:, :, None, :, :], -jnp.inf)
        A = jnp.einsum('bhtk,bhsk,bhtsk->bhts', qb, kb, jnp.exp(rel))
        o = jnp.einsum('bhtk,bhkv->bhtv', qb * jnp.exp(b), S) + jnp.einsum('bhts,bhsv->bhtv', A, vb)
        b_last = b[:, :, -1, :]
        S = jnp.exp(b_last)[..., None] * S + jnp.einsum(
            'bhsk,bhsv->bhkv', kb * jnp.exp(b_last[:, :, None, :] - b), vb)
        return S, o

    S, o = lax.scan(step, s0.astype(jnp.float32), (qc, kc, vc, gc))
    o = o.transpose(1, 0, 3, 2, 4).reshape(B, n * C, H, DV)[:, :T]
    return o, S


def dilated_attention(qs, ks, vs, offsets, block):
    B, T, H, hd = qs[0].shape
    nb = T // block
    scale = DIL_HD ** -0.5

    def one_block(bi):
        t0 = bi * block
        outs, lses = [], []
        for (w, r), q, k, v, off in zip(DIL_GROUPS, qs, ks, vs, offsets):
            n_keys = w // r + 1
            qb = lax.dynamic_slice_in_dim(q, t0, block, axis=1).astype(jnp.float32)
            idx = off + t0 + jnp.arange(block)[:, None] - r * jnp.arange(n_keys)[None, :]
            valid = idx >= 0
            idx = jnp.maximum(idx, 0)
            kg = jnp.take(k, idx, axis=1).astype(jnp.float32)
            vg = jnp.take(v, idx, axis=1).astype(jnp.float32)
            s = jnp.einsum('bthd,btnhd->bthn', qb, kg) * scale
            s = jnp.where(valid[None, :, None, :], s, -1e30)
            m = jnp.max(s, -1, keepdims=True)
            pe = jnp.exp(s - m)
            den = jnp.sum(pe, -1)
            outs.append(jnp.einsum('bthn,btnhd->bthd', pe, vg) / den[..., None])
            lses.append(m[..., 0] + jnp.log(den))
        wts = jax.nn.softmax(jnp.stack(lses, 0), axis=0)
        return jnp.sum(wts[..., None] * jnp.stack(outs, 0), axis=0)

    out = lax.map(one_block, jnp.arange(nb))
    return out.transpose(1, 0, 2, 3, 4).reshape(B, T, H, hd)


def conv_ffn(x, conv_prev, w_up, conv_w, conv_b, w_down):
    a, u = jnp.split(x @ w_up, 2, axis=-1)
    T = a.shape[1]
    ext = jnp.concatenate([conv_prev.astype(a.dtype), a], axis=1)
    c = conv_b
    for j in range(CONV_W):
        c = c + conv_w[j] * ext[:, j:j + T]
    y = (jax.nn.gelu(c, approximate=False) * u) @ w_down
    return y, ext[:, T:]


def trunk_layer(x, pe, pos, s0, conv_prev, kv_cache, q_block,
                w_in, w_gk_b, b_gk, gla_norm, w_br_gla, w_br_dil, w_out, ln1_g, ln1_b,
                w_up, conv_w, conv_b, w_down, ln2_g, ln2_b, w_ple_gate, w_ple_proj, ln3_g, ln3_b):
    B, T, _ = x.shape
    h = x @ w_in
    gq, gk, gv, gg, g_lr, dq, dk, dv, gate_a, gate_b = jnp.split(h, SPLIT_AT, axis=-1)

    logd = jax.nn.log_sigmoid((g_lr @ w_gk_b + b_gk).astype(jnp.float32)) / GLA_NORMALIZER
    o, S = gla_recurrence(gq.reshape(B, T, GLA_HEADS, GLA_DK) * (GLA_DK ** -0.5),
                          gk.reshape(B, T, GLA_HEADS, GLA_DK),
                          gv.reshape(B, T, GLA_HEADS, GLA_DV),
                          logd.reshape(B, T, GLA_HEADS, GLA_DK), s0)
    o = o * lax.rsqrt(jnp.mean(jnp.square(o), -1, keepdims=True) + NORM_EPS) * gla_norm.astype(jnp.float32)
    o = o * jax.nn.silu(gg.reshape(B, T, GLA_HEADS, GLA_DV).astype(jnp.float32))
    ya = o.reshape(B, T, GLA_V).astype(x.dtype) @ w_br_gla

    dq = dq.reshape(B, T, N_GROUPS, DIL_HEADS, DIL_HD)
    dk = dk.reshape(B, T, N_GROUPS, DIL_HEADS, DIL_HD)
    dv = dv.reshape(B, T, N_GROUPS, DIL_HEADS, DIL_HD)
    qs = [rope(dq[:, :, g], pos) for g in range(N_GROUPS)]
    ks = [rope(dk[:, :, g], pos) for g in range(N_GROUPS)]
    vs = [dv[:, :, g] for g in range(N_GROUPS)]
    if kv_cache is None:
        keys, vals, offs = ks, vs, [0] * N_GROUPS
        new_kv = [jnp.stack([k, v], 2)[:, -min(w, T):] for (w, _), k, v in zip(DIL_GROUPS, ks, vs)]
    else:
        keys = [jnp.concatenate([c[:, :, 0].astype(k.dtype), k], 1) for c, k in zip(kv_cache, ks)]
        vals = [jnp.concatenate([c[:, :, 1].astype(v.dtype), v], 1) for c, v in zip(kv_cache, vs)]
        offs = [c.shape[1] for c in kv_cache]
        new_kv = [jnp.stack([k, v], 2) for k, v in zip(ks, vs)]
    yb = dilated_attention(qs, keys, vals, offs, q_block)
    yb = yb.reshape(B, T, DIL_WIDTH).astype(x.dtype) @ w_br_dil

    m = jax.nn.sigmoid(gate_a) * ya + jax.nn.sigmoid(gate_b) * yb
    x = layer_norm(ALPHA * x + m @ w_out, ln1_g, ln1_b)

    f, conv_new = conv_ffn(x, conv_prev, w_up, conv_w, conv_b, w_down)
    x = layer_norm(ALPHA * x + f, ln2_g, ln2_b)

    x = layer_norm(ALPHA * x + jax.nn.sigmoid(x @ w_ple_gate) * (pe @ w_ple_proj), ln3_g, ln3_b)
    return x, S.astype(s0.dtype), conv_new, new_kv


def setup_inputs(seed: int = 0) -> dict:
    key = jax.random.key(seed)
    ks = jax.random.split(key, 28)
    f32 = jnp.float32

    def nrm(k, shape, scale=1.0):
        return jax.random.normal(k, shape, f32) * scale

    n_kv = [min(w, PAST_LEN) for w, _ in DIL_GROUPS]
    return {
        'x_prompt': nrm(ks[0], (BATCH, SEQ, D_MODEL)),
        'x_sample': nrm(ks[1], (DEC_BATCH, DEC_SEQ, D_MODEL)),
        'p_prompt': nrm(ks[2], (DEPTH, BATCH, SEQ, PLE_DIM)),
        'p_sample': nrm(ks[3], (DEPTH, DEC_BATCH, DEC_SEQ, PLE_DIM)),
        'state_gla': nrm(ks[4], (DEPTH, DEC_BATCH, GLA_HEADS, GLA_DK, GLA_DV), 0.5),
        'cache_conv': nrm(ks[5], (DEPTH, DEC_BATCH, CONV_W - 1, D_FF)),
        'cache_kv_w128': nrm(ks[6], (DEPTH, DEC_BATCH, n_kv[0], 2, DIL_HEADS, DIL_HD)),
        'cache_kv_w512': nrm(ks[7], (DEPTH, DEC_BATCH, n_kv[1], 2, DIL_HEADS, DIL_HD)),
        'cache_kv_w2048': nrm(ks[8], (DEPTH, DEC_BATCH, n_kv[2], 2, DIL_HEADS, DIL_HD)),
        'w_in': nrm(ks[9], (DEPTH, D_MODEL, IN_COLS), D_MODEL ** -0.5),
        'w_gk_b': nrm(ks[10], (DEPTH, GLA_RANK, GLA_QK), GLA_RANK ** -0.5),
        'b_gk': nrm(ks[11], (DEPTH, GLA_QK), 0.1),
        'gla_norm': 1.0 + nrm(ks[12], (DEPTH, GLA_DV), 0.02),
        'w_br_gla': nrm(ks[13], (DEPTH, GLA_V, D_MODEL), GLA_V ** -0.5),
        'w_br_dil': nrm(ks[14], (DEPTH, DIL_WIDTH, D_MODEL), DIL_WIDTH ** -0.5),
        'w_out': nrm(ks[15], (DEPTH, D_MODEL, D_MODEL), BETA * D_MODEL ** -0.5),
        'ln1_g': 1.0 + nrm(ks[16], (DEPTH, D_MODEL), 0.02),
        'ln1_b': nrm(ks[17], (DEPTH, D_MODEL), 0.02),
        'w_up': nrm(ks[18], (DEPTH, D_MODEL, 2 * D_FF), D_MODEL ** -0.5),
        'conv_w': nrm(ks[19], (DEPTH, CONV_W, D_FF), CONV_W ** -0.5),
        'conv_b': nrm(ks[20], (DEPTH, D_FF), 0.02),
        'w_down': nrm(ks[21], (DEPTH, D_FF, D_MODEL), BETA * D_FF ** -0.5),
        'ln2_g': 1.0 + nrm(ks[22], (DEPTH, D_MODEL), 0.02),
        'ln2_b': nrm(ks[23], (DEPTH, D_MODEL), 0.02),
        'w_ple_gate': nrm(ks[24], (DEPTH, D_MODEL, D_MODEL), D_MODEL ** -0.5),
        'w_ple_proj': nrm(ks[25], (DEPTH, PLE_DIM, D_MODEL), BETA * PLE_DIM ** -0.5),
        'ln3_g': 1.0 + nrm(ks[26], (DEPTH, D_MODEL), 0.02),
        'ln3_b': nrm(ks[27], (DEPTH, D_MODEL), 0.02),
    }


def reference(x_prompt, x_sample, p_prompt, p_sample, state_gla, cache_conv,
              cache_kv_w128, cache_kv_w512, cache_kv_w2048,
              w_in, w_gk_b, b_gk, gla_norm, w_br_gla, w_br_dil, w_out, ln1_g, ln1_b,
              w_up, conv_w, conv_b, w_down, ln2_g, ln2_b, w_ple_gate, w_ple_proj, ln3_g, ln3_b):
    Bp, Tp, _ = x_prompt.shape
    Ts = x_sample.shape[1]
    pos_p = jnp.arange(Tp, dtype=jnp.int32)
    pos_s = PAST_LEN + jnp.arange(Ts, dtype=jnp.int32)
    y_prompt, y_sample = x_prompt, x_sample
    gla_p, gla_s, conv_p, conv_s = [], [], [], []
    kv_p = [[] for _ in range(N_GROUPS)]
    kv_s = [[] for _ in range(N_GROUPS)]
    for i in range(DEPTH):
        lw = (w_in[i], w_gk_b[i], b_gk[i], gla_norm[i], w_br_gla[i], w_br_dil[i], w_out[i],
              ln1_g[i], ln1_b[i], w_up[i], conv_w[i], conv_b[i], w_down[i], ln2_g[i], ln2_b[i],
              w_ple_gate[i], w_ple_proj[i], ln3_g[i], ln3_b[i])
        s0 = jnp.zeros((Bp, GLA_HEADS, GLA_DK, GLA_DV), state_gla.dtype)
        c0 = jnp.zeros((Bp, CONV_W - 1, D_FF), x_prompt.dtype)
        y_prompt, sp, cp, kvp = trunk_layer(y_prompt, p_prompt[i], pos_p, s0, c0, None, Q_BLOCK, *lw)
        y_sample, ss, cs, kvs = trunk_layer(
            y_sample, p_sample[i], pos_s, state_gla[i], cache_conv[i],
            (cache_kv_w128[i], cache_kv_w512[i], cache_kv_w2048[i]), 1, *lw)
        gla_p.append(sp)
        gla_s.append(ss)
        conv_p.append(cp)
        conv_s.append(cs)
        for g in range(N_GROUPS):
            kv_p[g].append(kvp[g])
            kv_s[g].append(kvs[g])
    return (y_prompt, y_sample, jnp.stack(gla_p), jnp.stack(gla_s), jnp.stack(conv_p), jnp.stack(conv_s),
            jnp.stack(kv_p[0]), jnp.stack(kv_p[1]), jnp.stack(kv_p[2]),
            jnp.stack(kv_s[0]), jnp.stack(kv_s[1]), jnp.stack(kv_s[2]))
```

```python
import os
import numpy as np
from contextlib import ExitStack
import concourse.bass as bass
import concourse.mybir as mybir
from concourse.bass_utils import run_bass_kernel_spmd

F32 = mybir.dt.float32
BF16 = mybir.dt.bfloat16
AF = mybir.ActivationFunctionType
ALU = mybir.AluOpType

D = 1024
SEQ = 8192
NQ = 4
TPQ = 16
NHALO = 48
DK = 128
DV = 256
NH = 4
DFF = 2816
NFC = 22
ALPHA = 2.0 ** 0.25
EPS = 1e-5
GROUPS = ((128, 1), (512, 4), (2048, 16))
NDEL = (2, 5, 17)
C_Q, C_K, C_V, C_G, C_LR = 0, 512, 1024, 2048, 3072
C_DQ, C_DK, C_DV = 3088, 3088 + 1536, 3088 + 3072
C_GA, C_GB = 3088 + 4608, 3088 + 4608 + 1024
INCOLS = 9744


class StopBuild(Exception):
    pass


class Res:
    __slots__ = ("name", "lw", "rd")

    def __init__(self, name):
        self.name = name
        self.lw = None
        self.rd = []


class Eng:
    def __init__(self, name, h, sem):
        self.name = name
        self.h = h
        self.sem = sem
        self.n = 0
        self.seen = {}


class FW:
    def __init__(self, nc, es, n_dma_sems=40):
        self.nc = nc
        mk = lambda n: es.enter_context(nc.semaphore(n))
        self.pe = Eng("pe", nc.tensor, mk("s_pe"))
        self.act = Eng("act", nc.scalar, mk("s_act"))
        self.dve = Eng("dve", nc.vector, mk("s_dve"))
        self.pool = Eng("pool", nc.gpsimd, mk("s_pool"))
        self.sp = Eng("sp", nc.sync, mk("s_sp"))
        self.engs = [self.pe, self.act, self.dve, self.pool, self.sp]
        self.dsems_hw = [[mk(f"s_d{i}"), 0] for i in range(n_dma_sems)]
        self.dsems_sw = [[mk(f"s_w{i}"), 0] for i in range(24)]
        self.dsems = self.dsems_hw + self.dsems_sw
        self.dnext = {"hw": 0, "sw": 0}
        self.res = {}
        self.flip = 0

    def R(self, *key):
        r = self.res.get(key)
        if r is None:
            r = Res(key)
            self.res[key] = r
        return r

    def _deps(self, reads, writes):
        deps = {}

        def add(p):
            if p is None:
                return
            s, v = p
            k = id(s)
            if k not in deps or deps[k][1] < v:
                deps[k] = (s, v)
        for r in reads:
            add(r.lw)
            if r.name[0] in ("pr", "pl", "pb"):
                for p in r.rd:
                    add(p)
        for w in writes:
            add(w.lw)
            for p in w.rd:
                add(p)
        return deps

    def _wait(self, eng, deps, skip_self=False):
        for k, (s, v) in deps.items():
            if skip_self and s is eng.sem:
                continue
            if eng.seen.get(k, 0) >= v:
                continue
            eng.h.wait_ge(s, v)
            eng.seen[k] = v

    def op(self, eng, fn, reads=(), writes=(), skip_self=False):
        deps = self._deps(reads, writes)
        self._wait(eng, deps, skip_self=skip_self)
        ins = fn()
        ins.then_inc(eng.sem, 1)
        eng.n += 1
        tok = (eng.sem, eng.n)
        for w in writes:
            w.lw = tok
            w.rd = []
        for r in reads:
            r.rd.append(tok)
            if len(r.rd) > 16:
                best = {}
                for (s, v) in r.rd:
                    if id(s) not in best or best[id(s)][1] < v:
                        best[id(s)] = (s, v)
                r.rd = list(best.values())
        return ins

    def dma(self, q, out, in_, reads=(), writes=(), **kw):
        deps = self._deps(reads, writes)
        kind = "sw" if q is self.pool else "hw"
        pool_ = self.dsems_sw if kind == "sw" else self.dsems_hw
        slot = pool_[self.dnext[kind]]
        self.dnext[kind] = (self.dnext[kind] + 1) % len(pool_)
        s, v = slot
        if v > 0:
            deps[id(s)] = (s, v)
        self._wait(q, deps)
        ins = q.h.dma_start(out=out, in_=in_, **kw)
        ins.then_inc(s, 16)
        slot[1] = v + 16
        tok = (s, v + 16)
        for w in writes:
            w.lw = tok
            w.rd = []
        for r in reads:
            r.rd.append(tok)
        return ins

    def barrier(self):
        pts = [(e.sem, e.n) for e in self.engs if e.n > 0]
        pts += [(s, v) for (s, v) in self.dsems if v > 0]
        for e in self.engs:
            for (s, v) in pts:
                if s is e.sem or e.seen.get(id(s), 0) >= v:
                    continue
                e.h.wait_ge(s, v)
                e.seen[id(s)] = v

    def finish(self):
        e = self.sp
        pts = [(x.sem, x.n) for x in self.engs if x.n > 0 and x is not e]
        pts += [(s, v) for (s, v) in self.dsems if v > 0]
        for (s, v) in pts:
            if e.seen.get(id(s), 0) >= v:
                continue
            e.h.wait_ge(s, v)
            e.seen[id(s)] = v


def build_program():
    nc = bass.Bass("TRN2", target_bir_lowering=False)
    try:
        return _build(nc)
    except StopBuild:
        return nc


def _build(nc):
    di = lambda n, s: nc.dram_tensor(n, list(s), F32, kind="ExternalInput").ap()
    do = lambda n, s: nc.dram_tensor(n, list(s), F32, kind="ExternalOutput").ap()
    xh = di("xh", (64 * 128, D)); ph = di("ph", (17 * 128, 256))
    xs_d = di("xs", (128, D)); ps_d = di("psm", (128, 256))
    st_d = di("st", (16, NH, DK, DV)); cc_d = di("cc", (32, DFF))
    kvc_d = [di("kvc1", (16, 128, 1024)), di("kvc2", (16, 512, 1024)), di("kvc3", (16, 2048, 1024))]
    w_in = di("w_in", (D, INCOLS)); wgk_d = di("wgk", (32, 512)); gnorm_d = di("gnorm", (1, DV))
    w_brg = di("w_brg", (1024, D)); w_brd = di("w_brd", (512, D)); w_out = di("w_out", (D, D))
    lnp_d = di("lnp", (6, D)); w_up = di("w_up", (D, 2 * DFF)); cw_d = di("cw", (128, NFC * 4))
    w_down = di("w_down", (DFF, D)); w_pg = di("w_pg", (D, D)); w_pp = di("w_pp", (256, D))
    idf_d = di("idf", (128, 128)); amask_d = di("amask", (8, 128, 128)); flags_d = di("flags", (128, 2))
    um_d = di("um", (2, 3, 128, 128)); useg_d = di("useg", (128, 17)); rope_d = di("rope", (34, 128, 96))
    cmask_d = di("cmask", (128, 13 * 64)); nmask_d = di("nmask", (128, 16 * 3 * 64))
    bmask_d = di("bmask", (128, 16 * 128))
    y_p = do("y_p", (TPQ * 128, D)); y_s = do("y_s", (128, D))
    gs_p = do("gs_p", (NH, DK, DV)); gs_s = do("gs_s", (16, NH, DK, DV))
    cv_p = do("cv_p", (2, DFF)); cv_s = do("cv_s", (32, DFF))
    kvo = [do("kvo1", (128, 1024)), do("kvo2", (512, 1024)), do("kvo3", (2048, 1024))]
    kvs = [do("kvs1", (128, 1024)), do("kvs2", (128, 1024)), do("kvs3", (128, 1024))]

    wsrc = {"w_in": (w_in, D, INCOLS), "w_brg": (w_brg, 1024, D), "w_brd": (w_brd, 512, D), "w_out": (w_out, D, D),
            "w_up": (w_up, D, 2 * DFF), "w_down": (w_down, DFF, D), "w_pg": (w_pg, D, D), "w_pp": (w_pp, 256, D)}
    wscr = {k: nc.dram_tensor("scr_" + k, [v[1], v[2]], BF16, kind="Internal").ap() for k, v in wsrc.items()}
    es = ExitStack()
    with es:
        fw = FW(nc, es)
        R = fw.R
        pe, act, dve, pool, sp = fw.pe, fw.act, fw.dve, fw.pool, fw.sp
        _early = int(os.environ.get("K_EARLY", "0"))

        def ck(n):
            if _early == n:
                fw.finish()
                raise StopBuild()
        T = lambda n, s, d=F32: es.enter_context(nc.sbuf_tensor("sb_" + n, list(s), d))
        pes = ExitStack()
        Tp = lambda n, s, d=F32: pes.enter_context(nc.sbuf_tensor("sb_" + n, list(s), d))
        PL = [es.enter_context(nc.psum_tensor(f"pl{i}", [128, 512], F32)) for i in range(2)]
        PR = [es.enter_context(nc.psum_tensor(f"pr{i}", [128, 512], F32)) for i in range(5)]
        PB = [es.enter_context(nc.psum_tensor(f"pb{i}", [128, 1024], BF16)) for i in range(1)]
        rot = {"i": 0, "b": 0}

        def pr():
            i = rot["i"]; rot["i"] = (i + 1) % len(PR)
            return PR[i], R("pr", i)

        def pb():
            return PB[0], R("pb", 0)

        def mm(out, lhsT, rhs, start, stop, reads, writes):
            fw.op(pe, lambda: nc.tensor.matmul(out, lhsT, rhs, start=start, stop=stop),
                  reads=reads, writes=writes, skip_self=True)

        def tp(out, in_, ident, reads, writes):
            fw.op(pe, lambda: nc.tensor.transpose(out, in_, ident), reads=reads, writes=writes, skip_self=True)

        def evac(dst, src, reads, writes, eng=None):
            if eng is None:
                fw.flip ^= 1
                eng = act if fw.flip else dve
            if eng is act:
                fw.op(act, lambda: nc.scalar.copy(dst, src), reads=reads, writes=writes)
            else:
                fw.op(dve, lambda: nc.vector.tensor_copy(dst, src), reads=reads, writes=writes)

        def A(eng_fn_out, *a, **k):
            pass

        def actf(out, in_, func, reads, writes, **kw):
            fw.op(act, lambda: nc.scalar.activation(out, in_, func, **kw), reads=reads, writes=writes)

        def tt(out, in0, in1, op, reads, writes):
            fw.op(dve, lambda: nc.vector.tensor_tensor(out, in0, in1, op=op), reads=reads, writes=writes)

        def ts(out, in0, s1, s2, op0, op1, reads, writes):
            if op1 is None:
                fw.op(dve, lambda: nc.vector.tensor_scalar(out, in0, s1, None, op0=op0), reads=reads, writes=writes)
            else:
                fw.op(dve, lambda: nc.vector.tensor_scalar(out, in0, s1, s2, op0=op0, op1=op1), reads=reads, writes=writes)

        def stt(out, in0, scalar, in1, op0, op1, reads, writes):
            fw.op(dve, lambda: nc.vector.scalar_tensor_tensor(out, in0, scalar, in1, op0=op0, op1=op1),
                  reads=reads, writes=writes)

        idf = T("idf", (128, 128)); idb = T("idb", (128, 128), BF16)
        flags = T("flags", (128, 2))
        um = T("um", (128, 2, 3, 128)); useg = T("useg", (128, 17))
        maskA = T("maskA", (128, 2, 128), BF16)
        wgk = T("wgk", (32, 512)); gnorm = T("gnorm", (128, DV))
        lnp = T("lnp", (128, 2, D)); cw = T("cw", (128, NFC, 4))
        onesb = T("onesb", (128, 128), BF16)
        RC = R("const")
        fw.dma(sp, idf[:], idf_d, writes=[RC])
        fw.dma(sp, flags[:], flags_d, writes=[RC])
        fw.dma(sp, um[:], um_d.rearrange("k j s t -> s k j t"), writes=[RC])
        fw.dma(sp, useg[:], useg_d, writes=[RC])
        fw.dma(sp, wgk[:], wgk_d, writes=[RC])
        fw.dma(sp, gnorm[:], gnorm_d.partition_broadcast(128), writes=[RC])
        fw.dma(sp, cw[:], cw_d.rearrange("p (f j) -> p f j", j=4), writes=[RC])
        evac(idb[:], idf[:], [RC], [RC], eng=dve)
        for k in range(2):
            ts(maskA[:, k, :], um[:, k, 0, :], 0.0, None, ALU.not_equal, None, [RC], [RC])
        fw.op(dve, lambda: nc.vector.memset(onesb[:], 1.0), writes=[RC])

        conv_pending = []
        for k_ in ("w_in", "w_up", "w_down", "w_brg", "w_brd", "w_out", "w_pg", "w_pp"):
            src_, rows_, cols_ = wsrc[k_]
            r0 = 0
            while r0 < rows_:
                rn = min(1024, rows_ - r0)
                c0 = 0
                while c0 < cols_:
                    cn = min(2048, cols_ - c0)
                    conv_pending.append((k_, r0, rn, c0, cn))
                    c0 += cn
                r0 += rn

        def conv_issue(n_):
            for _ in range(n_):
                if not conv_pending:
                    return
                k_, r0, rn, c0, cn = conv_pending.pop(0)
                fw.dma(pool, wscr[k_][r0:r0 + rn, c0:c0 + cn], wsrc[k_][0][r0:r0 + rn, c0:c0 + cn],
                       writes=[R("wscr", k_, r0, c0)])

        conv_issue(2)
        ck(1)
        G = int(os.environ.get("K_G", "2"))
        NMAX = G * 128
        NWB = 4
        wbuf = [T(f"wb{i}", (128, 8, 512), BF16) for i in range(NWB)]
        wst = {"i": 0}
        gtmp = T("gtmp", (128, 512))
        AT = T("AT", (128, 4, 128), BF16)
        ssq = T("ssq", (128, 8))
        junk = T("junk", (128, 256)) if os.environ.get("K_T1") else gtmp
        ropet = [T(f"ropet{i}", (128, 96)) for i in range(2)]
        rst = {"i": 0}
        rA = T("rA", (128, 512)); rB = T("rB", (128, 512))
        sg = rA; t1 = rB
        if os.environ.get("K_T2"):
            sg = T("sg", (128, 512)); t1 = T("t1", (128, 512))
        kvout = [T(f"kvout{i}", (128, 2, 512)) for i in range(1)]
        kvst = {"i": 0}
        qr = T("qr", (128, 512))
        PT = [T(f"PT{i}", (128, 4, 128), BF16) for i in range(4)]
        ptst = {"i": 0}
        rden = T("rden", (128, 64))
        attnb = T("attnb", (128, 512), BF16)
        stats = T("stats", (128, 2, 6)); mv = T("mv", (128, 2)); rs = T("rs", (128, 2))
        aest = {"i": 0}
        pin = T("pin", (128, 256))
        cvo = rA

        def alloc_group_bufs(TT, g_):
            n_ = g_ * 128
            d = {}
            d["xres"] = TT("xres", (128, g_, D)); d["xT"] = TT("xT", (128, 8, n_), BF16)
            d["glrT"] = TT("glrT", (32, n_))
            d["spb"] = TT("spb", (128, g_, 512)); d["EbT"] = TT("EbT", (128, g_, 512), BF16)
            d["EnbT"] = TT("EnbT", (128, g_, 512), BF16)
            d["Ebl"] = TT("Ebl", (128, g_, 512), BF16); d["eblT"] = TT("eblT", (128, g_, 64))
            d["qtT"] = TT("qtT", (128, g_, 4, 128), BF16); d["ktT"] = TT("ktT", (128, g_, 4, 128), BF16)
            d["khat"] = TT("khat", (128, g_, 512), BF16); d["vb"] = TT("vb", (128, g_, 1024), BF16)
            d["ggn"] = TT("ggn", (128, g_, 1024), BF16)
            d["onT"] = TT("onT", (128, 8, n_), BF16); d["onb"] = TT("onb", (128, g_, 1024), BF16)
            d["qT"] = [TT(f"qT{g}", (128, g_, 4, 128), BF16) for g in range(3)]
            d["attnT"] = TT("attnT", (128, 4, n_), BF16)
            d["mb"] = d["ggn"]; d["mT"] = d["onT"]
            d["cacc"] = TT("cacc", (128, n_)); d["gl"] = TT("gl", (128, n_))
            d["yT"] = TT("yT", (128, NFC, n_), BF16); d["pT"] = TT("pT", (128, 2, n_), BF16)
            return d

        _gb = alloc_group_bufs(lambda n, s, d=F32: Tp("g_" + n, s, d), G)
        xres = _gb["xres"]; xT = _gb["xT"]; glrT = _gb["glrT"]; spb = _gb["spb"]; EbT = _gb["EbT"]; EnbT = _gb["EnbT"]
        Ebl = _gb["Ebl"]; eblT = _gb["eblT"]; qtT = _gb["qtT"]; ktT = _gb["ktT"]; khat = _gb["khat"]; vb = _gb["vb"]
        ggn = _gb["ggn"]; onT = _gb["onT"]; onb = _gb["onb"]; qT = _gb["qT"]; attnT = _gb["attnT"]; mb = _gb["mb"]; mT = _gb["mT"]
        cacc = _gb["cacc"]; gl = _gb["gl"]; yT = _gb["yT"]; pT = _gb["pT"]
        amask = Tp("amask", (128, 8, 128), BF16); amaskp = Tp("amaskp", (128, 8, 128), BF16)
        amaskq = Tp("amaskq", (128, 128), BF16)
        Sst = Tp("Sst", (128, NH, DV)); Sbf = Tp("Sbf", (128, NH, DV), BF16)
        xbf = Tp("xbf", (128, G, D), BF16)
        aext = [Tp(f"aext{i}", (128, 2 + NMAX)) for i in range(2)]
        tails = Tp("tails", (128, NFC, 2))
        _pad = int(os.environ.get("K_PAD", "0"))
        if _pad:
            padbuf = Tp("padbuf", (128, _pad))
        RS = [NDEL[g] + G - 1 for g in range(3)]
        kTr = [Tp(f"kTr{g}", (128, RS[g], 4, 128), BF16) for g in range(3)]
        Vr = [Tp(f"Vr{g}", (128, RS[g], 8, 66), BF16) for g in range(3)]
        for m4 in range(2):
            ctmp = gtmp[:].rearrange("p (m t) -> p m t", m=4)
            fw.dma(sp, ctmp, amask_d[m4 * 4:(m4 + 1) * 4].rearrange("m s t -> s m t"), writes=[R("gtmp")])
            evac(amask[:, m4 * 4:(m4 + 1) * 4, :], ctmp, [R("gtmp")], [RC], eng=dve)
            ts(amaskp[:, m4 * 4:(m4 + 1) * 4, :], ctmp, flags[:, 0:1], None, ALU.mult, None, [R("gtmp"), RC], [RC])
            if m4 == 1:
                ts(amaskq[:], ctmp[:, 3, :], flags[:, 1:2], None, ALU.mult, None, [R("gtmp"), RC], [RC])
        for g in range(3):
            fw.op(dve, lambda g=g: nc.vector.memset(Vr[g][:], 1.0), writes=[R("Vr", g, s_) for s_ in range(RS[g])])

        fw.op(dve, lambda: nc.vector.memset(glrT[:], 1.0), writes=[R("glrT")])
        fw.op(dve, lambda: nc.vector.memset(Sst[:], 0.0), writes=[R("S")])
        fw.op(dve, lambda: nc.vector.memset(Sbf[:], 0.0), writes=[R("Sbf")])
        fw.op(dve, lambda: nc.vector.memset(tails[:], 0.0), writes=[R("tails")])

        wname = {id(v[0]): k for k, v in wsrc.items()}

        def wload(w_ap, k0, nk, col0, cw_):
            i = wst["i"]; wst["i"] = (i + 1) % NWB
            buf = wbuf[i]
            k_ = wname[id(w_ap)]
            rows_, cols_ = wsrc[k_][1], wsrc[k_][2]
            src = wscr[k_][k0 * 128:(k0 + nk) * 128, col0:col0 + cw_].rearrange("(k p) c -> p k c", p=128)
            deps = []
            r0 = 0
            while r0 < rows_:
                c0 = 0
                while c0 < cols_:
                    if r0 < (k0 + nk) * 128 and r0 + 1024 > k0 * 128 and c0 < col0 + cw_ and c0 + 2048 > col0:
                        deps.append(R("wscr", k_, r0, c0))
                    c0 += 2048
                r0 += 1024
            fw.dma(pool, buf[:, 0:nk, 0:cw_], src, reads=deps, writes=[R("wb", i)])
            return buf, R("wb", i)

        def issue_x(tiles, src_fn, rope_idx=None):
            for j, t in enumerate(tiles):
                fw.dma(sp, xres[:, j, :], src_fn(t), writes=[R("xres", j)])

        def load_x(tiles, src_fn, rope_idx=None, issued=False):
            if not issued:
                issue_x(tiles, src_fn, rope_idx)
            if rope_idx is not None:
                for j, t in enumerate(tiles):
                    fw.dma(sp, ropet[j][:], rope_d[rope_idx(t)], writes=[R("ropet", j)])
            for j, t in enumerate(tiles):
                transpose_f32(lambda c, j=j: xres[:, j, c * 128:(c + 1) * 128], 8,
                              lambda c0, n, j=j: xT[:, c0:c0 + n, j * 128:(j + 1) * 128],
                              [R("xres", j)], [R("xT", j)])

        def transpose_f32(src_fn, nch, dst_fn, reads, writes):
            c = 0
            while c < nch:
                n = min(4, nch - c)
                ps, rp = pr()
                for i in range(n):
                    tp(ps[:, i * 128:(i + 1) * 128], src_fn(c + i), idf[:], reads + [RC], [rp])
                evac(dst_fn(c, n), ps[:, 0:n * 128].rearrange("p (a b) -> p a b", a=n), [rp], writes)
                c += n

        def transpose_b16(src_fn, nch, dst_fn, reads, writes):
            ps, rp = pb()
            for i in range(nch):
                tp(ps[:, i * 128:(i + 1) * 128], src_fn(i), idb[:], reads + [RC], [rp])
            evac(dst_fn(0, nch), ps[:, 0:nch * 128].rearrange("p (a b) -> p a b", a=nch), [rp], writes)

        def proj_tok(j, wl, rw, nk, cw_, kfn, reads):
            ps, rp = pr()
            for k in range(nk):
                mm(ps[:, 0:cw_], kfn(k), wl[:, k, 0:cw_], k == 0, k == nk - 1, reads + [rw], [rp])
            return ps, rp

        xTk = lambda j: (lambda k: xT[:, k, j * 128:(j + 1) * 128])

        def gla_prep(tiles, kinds, sample):
            n = len(tiles); N = n * 128
            ku = 1 if sample else 0
            nseg = 16 if sample else 1
            rx = [R("xT", j) for j in range(n)]
            anyfull = any(k == "full" for k in kinds)
            wl, rw = wload(w_in, 0, 8, C_LR, 16)
            ps, rp = pr()
            for k in range(8):
                mm(ps[0:16, 0:N], wl[:, k, 0:16], xT[:, k, 0:N], k == 0, k == 7, rx + [rw], [rp])
            evac(glrT[0:16, 0:N], ps[0:16, 0:N], [rp], [R("glrT")])
            for cb in range(2):
                wl, rw = wload(w_in, 0, 8, C_V + cb * 512, 512)
                for j in range(n):
                    ps, rp = proj_tok(j, wl, rw, 8, 512, xTk(j), [R("xT", j)])
                    evac(vb[:, j, cb * 512:(cb + 1) * 512], ps[:, 0:512], [rp], [R("vb", j)])
            for j in range(n):
                ps, rp = pr()
                mm(ps[:, 0:512], glrT[0:32, j * 128:(j + 1) * 128], wgk[0:32, :], True, True, [R("glrT"), RC], [rp])
                actf(spb[:, j, :], ps[:, 0:512], AF.Exp, [rp], [R("spb", j)], scale=-1.0)
                actf(spb[:, j, :], spb[:, j, :], AF.Ln, [R("spb", j)], [R("spb", j)], bias=1.0)
            if anyfull:
                for cb in range(2):
                    wl, rw = wload(w_in, 0, 8, C_G + cb * 512, 512)
                    for j in range(n):
                        if kinds[j] != "full":
                            continue
                        ps, rp = proj_tok(j, wl, rw, 8, 512, xTk(j), [R("xT", j)])
                        actf(gtmp[:, 0:512], ps[:, 0:512], AF.Silu, [rp], [R("gtmp")])
                        tt(ggn[:, j, cb * 512:(cb + 1) * 512].rearrange("p (h v) -> p h v", h=2),
                           gtmp[:, 0:512].rearrange("p (h v) -> p h v", h=2),
                           gnorm[:].unsqueeze(1).broadcast_to([128, 2, DV]), ALU.mult,
                           [R("gtmp"), RC], [R("ggn", j)])
            for j in range(n):
                if kinds[j] == "full":
                    ps, rp = pr()
                    for h in range(4):
                        mm(ps[:, h * 128:(h + 1) * 128], spb[:, j, h * 128:(h + 1) * 128], um[:, ku, 0, :], True, True,
                           [R("spb", j), RC], [rp])
                    actf(EbT[:, j, :], ps[:, 0:512], AF.Exp, [rp], [R("EbT", j)])
                    actf(EnbT[:, j, :], ps[:, 0:512], AF.Exp, [rp], [R("EnbT", j)], scale=-1.0)
                ps, rp = pr()
                mm(ps[:, 0:512], um[:, ku, 1, :], spb[:, j, :], True, True, [R("spb", j), RC], [rp])
                actf(Ebl[:, j, :], ps[:, 0:512], AF.Exp, [rp], [R("Ebl", j)])
                ps, rp = pr()
                for h in range(4):
                    rhs = useg[:, 1:17] if sample else useg[:, 0:1]
                    mm(ps[:, h * 16:h * 16 + nseg], spb[:, j, h * 128:(h + 1) * 128], rhs, True, True,
                       [R("spb", j), RC], [rp])
                if sample:
                    actf(eblT[:, j, :], ps[:, 0:64], AF.Exp, [rp], [R("eblT", j)])
                else:
                    for h in range(4):
                        actf(eblT[:, j, h * 16:h * 16 + 1], ps[:, h * 16:h * 16 + 1], AF.Exp, [rp], [R("eblT", j)])
            wl, rw = wload(w_in, 0, 8, C_K, 512)
            if anyfull:
                for h in range(4):
                    ps, rp = pr()
                    for k in range(8):
                        mm(ps[:, 0:N], wl[:, k, h * 128:(h + 1) * 128], xT[:, k, 0:N], k == 0, k == 7, rx + [rw], [rp])
                    for j in range(n):
                        if kinds[j] == "full":
                            tt(ktT[:, j, h, :], ps[:, j * 128:(j + 1) * 128], EnbT[:, j, h * 128:(h + 1) * 128], ALU.mult,
                               [rp, R("EnbT", j)], [R("ktT", j)])
            for j in range(n):
                ps, rp = proj_tok(j, wl, rw, 8, 512, xTk(j), [R("xT", j)])
                tt(khat[:, j, :], ps[:, 0:512], Ebl[:, j, :], ALU.mult, [rp, R("Ebl", j)], [R("khat", j)])
            if anyfull:
                wl, rw = wload(w_in, 0, 8, C_Q, 512)
                for h in range(4):
                    ps, rp = pr()
                    for k in range(8):
                        mm(ps[:, 0:N], wl[:, k, h * 128:(h + 1) * 128], xT[:, k, 0:N], k == 0, k == 7, rx + [rw], [rp])
                    for j in range(n):
                        if kinds[j] == "full":
                            stt(qtT[:, j, h, :], ps[:, j * 128:(j + 1) * 128], float(DK ** -0.5),
                                EbT[:, j, h * 128:(h + 1) * 128], ALU.mult, ALU.mult,
                                [rp, R("EbT", j)], [R("qtT", j)])

        def gla_proj(tiles, kinds, sample):
            return

        def gla_out_norm(j):
            rl = [R("pl", 0), R("pl", 1)]
            for h in range(4):
                fw.op(act, lambda h=h: nc.scalar.activation(junk[:, 0:256], PL[h // 2][:, (h % 2) * 256:(h % 2 + 1) * 256],
                                                            AF.Square, accum_out=ssq[:, h:h + 1]),
                      reads=[rl[h // 2]], writes=[R("gtmp"), R("ssq")])
            actf(ssq[:, 4:8], ssq[:, 0:4], AF.Ln, [R("ssq")], [R("ssq")], scale=1.0 / DV, bias=EPS)
            actf(ssq[:, 4:8], ssq[:, 4:8], AF.Exp, [R("ssq")], [R("ssq")], scale=-0.5)
            for h in range(4):
                stt(onb[:, j, h * 256:(h + 1) * 256], PL[h // 2][:, (h % 2) * 256:(h % 2 + 1) * 256], ssq[:, 4 + h:5 + h],
                    ggn[:, j, h * 256:(h + 1) * 256], ALU.mult, ALU.mult,
                    [rl[h // 2], R("ssq"), R("ggn", j)], [R("onb", j)])

        def gla_out_T(j):
            transpose_b16(lambda c: onb[:, j, c * 128:(c + 1) * 128], 8,
                          lambda c0, n_: onT[:, c0:c0 + n_, j * 128:(j + 1) * 128], [R("onb", j)], [R("onT", j)])

        def gla_seq_prompt(tiles, kinds):
            for j, t in enumerate(tiles):
                if kinds[j] == "full":
                    ps, rp = pr()
                    for h in range(4):
                        mm(ps[:, h * 128:(h + 1) * 128], ktT[:, j, h, :], qtT[:, j, h, :], True, True,
                           [R("ktT", j), R("qtT", j)], [rp])
                    tt(AT[:].rearrange("p h t -> p h t"), ps[:, 0:512].rearrange("p (h t) -> p h t", h=4),
                       maskA[:, 0, :].unsqueeze(1).broadcast_to([128, 4, 128]), ALU.mult, [rp, RC], [R("AT")])
                    for h in range(4):
                        o = PL[h // 2][:, (h % 2) * 256:(h % 2 + 1) * 256]
                        rl = R("pl", h // 2)
                        mm(o, AT[:, h, :], vb[:, j, h * 256:(h + 1) * 256], h % 2 == 0, False, [R("AT"), R("vb", j)], [rl])
                        mm(o, qtT[:, j, h, :], Sbf[:, h, :], False, h % 2 == 1, [R("qtT", j), R("Sbf")], [rl])
                for hp in range(2):
                    ps, rp = pr()
                    for e in range(2):
                        h = hp * 2 + e
                        mm(ps[:, e * 256:(e + 1) * 256], khat[:, j, h * 128:(h + 1) * 128], vb[:, j, h * 256:(h + 1) * 256],
                           True, True, [R("khat", j), R("vb", j)], [rp])
                    for e in range(2):
                        h = hp * 2 + e
                        stt(Sst[:, h, :], Sst[:, h, :], eblT[:, j, h * 16:h * 16 + 1], ps[:, e * 256:(e + 1) * 256],
                            ALU.mult, ALU.add, [rp, R("eblT", j), R("S")], [R("S")])
                evac(Sbf[:], Sst[:], [R("S")], [R("Sbf")], eng=act)
                if kinds[j] == "full":
                    gla_out_norm(j)

        def rope_to(dst, ps, rp, rt, rrt, writes):
            X = ps[:, 0:512].rearrange("p (h a d) -> p h a d", h=8, a=2)
            cosb = rt[:, 0:32].unsqueeze(1).unsqueeze(1).broadcast_to([128, 8, 2, 32])
            sinb = rt[:, 32:64].unsqueeze(1).broadcast_to([128, 8, 32])
            nsinb = rt[:, 64:96].unsqueeze(1).broadcast_to([128, 8, 32])
            tt(rA[:].rearrange("p (h a d) -> p h a d", h=8, a=2), X, cosb, ALU.mult, [rp, rrt], [R("rA")])
            Bv = rB[:].rearrange("p (h a d) -> p h a d", h=8, a=2)
            tt(Bv[:, :, 0, :], X[:, :, 1, :], nsinb, ALU.mult, [rp, rrt], [R("rB")])
            tt(Bv[:, :, 1, :], X[:, :, 0, :], sinb, ALU.mult, [rp, rrt], [R("rB")])
            tt(dst, rA[:], rB[:], ALU.add, [R("rA"), R("rB")], writes)

        KV = {}

        def attn_kv(tiles, kinds, sample, rope_idx, kv_dst):
            n = len(tiles)
            for g in range(3):
                need = [kinds[j] == "full" or (kinds[j] == "kv" and tiles[j] >= -1 - GROUPS[g][0] // 128) for j in range(n)]
                if not any(need):
                    continue
                wk, rwk = wload(w_in, 0, 8, C_DK + g * 512, 512)
                wv, rwv = wload(w_in, 0, 8, C_DV + g * 512, 512)
                anyfull = any(k == "full" for k in kinds)
                if anyfull:
                    wq, rwq = wload(w_in, 0, 8, C_DQ + g * 512, 512)
                for j, t in enumerate(tiles):
                    if not need[j]:
                        continue
                    rt = ropet[j]; rrt = R("ropet", j)
                    ko = 0
                    kvt = kvout[ko]; rkv = R("kvout", ko)
                    slot = 0 if sample else (t % RS[g])
                    kTr_ = KV["k"]; Vr_ = KV["v"]
                    psk, rpk = proj_tok(j, wk, rwk, 8, 512, xTk(j), [R("xT", j)])
                    isfull = kinds[j] == "full"
                    if isfull:
                        psq, rpq = proj_tok(j, wq, rwq, 8, 512, xTk(j), [R("xT", j)])
                    psv, rpv = proj_tok(j, wv, rwv, 8, 512, xTk(j), [R("xT", j)])
                    rope_to(kvt[:, 0, :], psk, rpk, rt, rrt, [rkv])
                    evac(kvt[:, 1, :], psv[:, 0:512], [rpv], [rkv], eng=act)
                    if isfull:
                        rope_to(qr[:], psq, rpq, rt, rrt, [R("qr")])
                    evac(Vr_[g][:, slot, :, 0:64], psv[:, 0:512].rearrange("p (h d) -> p h d", h=8), [rpv],
                         [R("Vr", g, slot)], eng=dve)
                    dst = kv_dst(g, t)
                    if dst is not None:
                        fw.dma(sp, dst, kvt[:].rearrange("p a c -> p (a c)"), reads=[rkv])
                    transpose_f32(lambda c: kvt[:, 0, c * 128:(c + 1) * 128], 4,
                                  lambda c0, n_: kTr_[g][:, slot, c0:c0 + n_, :], [rkv], [R("kTr", g, slot)])
                    if isfull:
                        transpose_f32(lambda c: qr[:, c * 128:(c + 1) * 128], 4,
                                      lambda c0, n_, j=j: qT[g][:, j, c0:c0 + n_, :], [R("qr")], [R("qT", g, j)])
                        if g == 2 and not sample:
                            attn_prompt(j, t)

        def attn_prompt(j, t):
            first = [True, True]
            rl = [R("pl", 0), R("pl", 1)]
            steps = []
            for g in range(3):
                for dl in range(NDEL[g]):
                    kt = t - dl
                    slot = kt % RS[g]
                    if g == 0:
                        mi = dl
                    elif g == 1:
                        mi = 2 + (0 if dl == 0 else (2 if dl == 4 else 1))
                    else:
                        mi = 5 + (0 if dl == 0 else (2 if dl == 16 else 1))
                    if kt == t or kt >= 0:
                        mk = amask[:, mi, :]
                    elif kt == -17:
                        mk = amaskq[:]
                    else:
                        mk = amaskp[:, mi, :]
                    for half in range(2):
                        steps.append((g, dl, slot, mk, half))

            def scores(st):
                g, dl, slot, mk, half = st
                e = half
                ps, rp = pr()
                for hh in range(4):
                    mm(ps[:, hh * 128:(hh + 1) * 128], kTr[g][e * 64:(e + 1) * 64, slot, hh, :],
                       qT[g][e * 64:(e + 1) * 64, j, hh, :], True, True, [R("kTr", g, slot), R("qT", g, j)], [rp])
                pi = ptst["i"]; ptst["i"] = (pi + 1) % len(PT)
                P_ = PT[pi]; rP = R("PT", pi)
                actf(P_[:].rearrange("p h t -> p (h t)"), ps[:, 0:512], AF.Exp, [rp], [rP], scale=0.125)
                tt(P_[:], P_[:], mk.unsqueeze(1).broadcast_to([128, 4, 128]), ALU.mult, [rP, RC], [rP])
                return P_, rP

            def pvmm(st, P_, rP, last):
                g, dl, slot, mk, half = st
                for hh in range(4):
                    h = 2 * hh + half
                    mm(PL[half][:, hh * 65:(hh + 1) * 65], P_[:, hh, :], Vr[g][:, slot, h, 0:65], first[half],
                       (last and hh == 3), [rP, R("Vr", g, slot)], [rl[half]])
                    first[half] = False

            pend = []
            ns_ = len(steps)
            for i_, st in enumerate(steps):
                P_, rP = scores(st)
                pend.append((i_, st, P_, rP))
                if len(pend) > 2:
                    i0, st0, P0, rP0 = pend.pop(0)
                    pvmm(st0, P0, rP0, i0 >= ns_ - 2)
            while pend:
                i0, st0, P0, rP0 = pend.pop(0)
                pvmm(st0, P0, rP0, i0 >= ns_ - 2)
            for half in range(2):
                pv = PL[half][:, 0:260].rearrange("p (h d) -> p h d", h=4)
                fw.op(dve, lambda half=half, pv=pv: nc.vector.reciprocal(rden[:, half * 4:half * 4 + 4], pv[:, :, 64]),
                      reads=[rl[half]], writes=[R("rden")])
                tt(attnb[:].rearrange("p (a e d) -> p a e d", a=4, e=2)[:, :, half, :], pv[:, :, 0:64],
                   rden[:, half * 4:half * 4 + 4].unsqueeze(2).broadcast_to([128, 4, 64]), ALU.mult,
                   [rl[half], R("rden")], [R("attnb")])
            transpose_b16(lambda c: attnb[:, c * 128:(c + 1) * 128], 4,
                          lambda c0, n_: attnT[:, c0:c0 + n_, j * 128:(j + 1) * 128], [R("attnb")], [R("attnT", j)])

        def ln_load(li):
            for i_ in range(2):
                fw.dma(sp, lnp[:, i_, :], lnp_d[2 * li + i_:2 * li + i_ + 1, :].partition_broadcast(128), writes=[R("lnp")])

        def layernorm(j, li, src_writes):
            rx = R("xres", j)
            for c in range(2):
                fw.op(dve, lambda c=c: nc.vector.bn_stats(stats[:, c, :], xres[:, j, c * 512:(c + 1) * 512]),
                      reads=[rx], writes=[R("stats")])
            fw.op(dve, lambda: nc.vector.bn_aggr(mv[:], stats[:].rearrange("p a b -> p (a b)")), reads=[R("stats")], writes=[R("mv")])
            actf(rs[:, 0:1], mv[:, 1:2], AF.Ln, [R("mv")], [R("rs")], bias=EPS)
            actf(rs[:, 0:1], rs[:, 0:1], AF.Exp, [R("rs")], [R("rs")], scale=-0.5)
            ts(xres[:, j, :], xres[:, j, :], mv[:, 0:1], rs[:, 0:1], ALU.subtract, ALU.mult, [rx, R("mv"), R("rs")], [rx])
            tt(xres[:, j, :], xres[:, j, :], lnp[:, 0, :], ALU.mult, [rx, R("lnp")], [rx])
            tt(xres[:, j, :], xres[:, j, :], lnp[:, 1, :], ALU.add, [rx, R("lnp")], [rx])

        def merge_out(tiles):
            n = len(tiles)
            ln_load(0)
            for j in range(n):
                gla_out_T(j)
            for cb in range(2):
                wga, rga = wload(w_in, 0, 8, C_GA + cb * 512, 512)
                wya, rya = wload(w_brg, 0, 8, cb * 512, 512)
                wgb, rgb = wload(w_in, 0, 8, C_GB + cb * 512, 512)
                wyb, ryb = wload(w_brd, 0, 4, cb * 512, 512)
                for j in range(n):
                    ps, rp = proj_tok(j, wga, rga, 8, 512, xTk(j), [R("xT", j)])
                    actf(sg[:], ps[:, 0:512], AF.Sigmoid, [rp], [R("rA")])
                    ps, rp = proj_tok(j, wya, rya, 8, 512, lambda k, j=j: onT[:, k, j * 128:(j + 1) * 128], [R("onT", j)])
                    tt(t1[:], ps[:, 0:512], sg[:], ALU.mult, [rp, R("rA")], [R("rB")])
                    ps, rp = proj_tok(j, wgb, rgb, 8, 512, xTk(j), [R("xT", j)])
                    actf(sg[:], ps[:, 0:512], AF.Sigmoid, [rp], [R("rA")])
                    ps, rp = proj_tok(j, wyb, ryb, 4, 512, lambda k, j=j: attnT[:, k, j * 128:(j + 1) * 128], [R("attnT", j)])
                    tt(sg[:], ps[:, 0:512], sg[:], ALU.mult, [rp, R("rA")], [R("rA")])
                    tt(mb[:, j, cb * 512:(cb + 1) * 512], sg[:], t1[:], ALU.add, [R("rA"), R("rB")], [R("ggn", j)])
            for j in range(n):
                transpose_b16(lambda c, j=j: mb[:, j, c * 128:(c + 1) * 128], 8,
                              lambda c0, n_, j=j: mT[:, c0:c0 + n_, j * 128:(j + 1) * 128], [R("ggn", j)], [R("onT", j)])
            for cb in range(2):
                wl, rw = wload(w_out, 0, 8, cb * 512, 512)
                for j in range(n):
                    ps, rp = proj_tok(j, wl, rw, 8, 512, lambda k, j=j: mT[:, k, j * 128:(j + 1) * 128], [R("onT", j)])
                    stt(xres[:, j, cb * 512:(cb + 1) * 512], xres[:, j, cb * 512:(cb + 1) * 512], float(ALPHA), ps[:, 0:512],
                        ALU.mult, ALU.add, [rp, R("xres", j)], [R("xres", j)])
            for j in range(n):
                layernorm(j, 0, None)
                transpose_f32(lambda c, j=j: xres[:, j, c * 128:(c + 1) * 128], 8,
                              lambda c0, n_, j=j: xT[:, c0:c0 + n_, j * 128:(j + 1) * 128], [R("xres", j)], [R("xT", j)])

        def ffn(tiles, sample, first_group, aext_s=None, pre_f4=None, post_f4=None):
            n = len(tiles); N = n * 128
            rx = [R("xT", j) for j in range(n)]
            ln_load(1)
            for f4 in range(6):
                nf = min(4, NFC - f4 * 4)
                wa, rwa = wload(w_up, 0, 8, f4 * 512, nf * 128)
                wu, rwu = wload(w_up, 0, 8, DFF + f4 * 512, nf * 128)
                if pre_f4 is not None:
                    pre_f4(f4, nf)
                for c in range(nf):
                    fc = f4 * 4 + c
                    ps, rp = pr()
                    for k in range(8):
                        mm(ps[:, 0:N], wa[:, k, c * 128:(c + 1) * 128], xT[:, k, 0:N], k == 0, k == 7, rx + [rwa], [rp])
                    if sample:
                        ae = aext_s[:, c, :, :]
                        rae = R("aexts", c)
                        evac(ae[:, :, 2:10], ps[:, 0:128].rearrange("p (b t) -> p b t", b=16), [rp], [rae], eng=act)
                        sh = lambda o: ae[:, :, o:o + 8]
                        cv = cacc[:, 0:128].rearrange("p (b t) -> p b t", b=16)
                    else:
                        ai = aest["i"]; aest["i"] ^= 1
                        ae = aext[ai]; rae = R("aext", ai)
                        evac(ae[:, 0:2], tails[:, fc, :], [R("tails")], [rae], eng=act)
                        evac(ae[:, 2:2 + N], ps[:, 0:N], [rp], [rae], eng=act)
                        if first_group:
                            ts(ae[:, 2:130], ae[:, 2:130], flags[:, 0:1], None, ALU.mult, None, [rae, RC], [rae])
                        evac(tails[:, fc, :], ae[:, N:N + 2], [rae], [R("tails")], eng=act)
                        sh = lambda o: ae[:, o:o + N]
                        cv = cacc[:, 0:N]
                    ts(cv, sh(0), cw[:, fc, 0:1], cw[:, fc, 3:4], ALU.mult, ALU.add, [rae, RC], [R("cacc")])
                    stt(cv, sh(1), cw[:, fc, 1:2], cv, ALU.mult, ALU.add, [rae, RC, R("cacc")], [R("cacc")])
                    stt(cv, sh(2), cw[:, fc, 2:3], cv, ALU.mult, ALU.add, [rae, RC, R("cacc")], [R("cacc")])
                    actf(gl[:, 0:N], cacc[:, 0:N], AF.Gelu, [R("cacc")], [R("gl")])
                    ps, rp = pr()
                    for k in range(8):
                        mm(ps[:, 0:N], wu[:, k, c * 128:(c + 1) * 128], xT[:, k, 0:N], k == 0, k == 7, rx + [rwu], [rp])
                    tt(yT[:, fc, 0:N], ps[:, 0:N], gl[:, 0:N], ALU.mult, [rp, R("gl")], [R("yT")])
                if post_f4 is not None:
                    post_f4(f4, nf)
            for cb in range(2):
                accs = [pr() for _ in range(n)]
                for fg in range(3):
                    k0 = fg * 8; nk = min(8, NFC - k0)
                    wl, rw = wload(w_down, k0, nk, cb * 512, 512)
                    for j in range(n):
                        ps, rp = accs[j]
                        for k in range(nk):
                            fc = k0 + k
                            mm(ps[:, 0:512], yT[:, fc, j * 128:(j + 1) * 128], wl[:, k, :], fc == 0, fc == NFC - 1,
                               [R("yT"), rw], [rp])
                for j in range(n):
                    ps, rp = accs[j]
                    stt(xres[:, j, cb * 512:(cb + 1) * 512], xres[:, j, cb * 512:(cb + 1) * 512], float(ALPHA), ps[:, 0:512],
                        ALU.mult, ALU.add, [rp, R("xres", j)], [R("xres", j)])
            for j in range(n):
                layernorm(j, 1, None)
                transpose_f32(lambda c, j=j: xres[:, j, c * 128:(c + 1) * 128], 8,
                              lambda c0, n_, j=j: xT[:, c0:c0 + n_, j * 128:(j + 1) * 128], [R("xres", j)], [R("xT", j)])

        def ple(tiles, p_src, y_dst, pin_pref=False):
            n = len(tiles)
            ln_load(2)
            for j, t in enumerate(tiles):
                if not (j == 0 and pin_pref):
                    fw.dma(sp, pin[:], p_src(t), writes=[R("pin")])
                transpose_f32(lambda c: pin[:, c * 128:(c + 1) * 128], 2,
                              lambda c0, n_, j=j: pT[:, c0:c0 + n_, j * 128:(j + 1) * 128], [R("pin")], [R("pT", j)])
            for cb in range(2):
                wg, rwg = wload(w_pg, 0, 8, cb * 512, 512)
                wp, rwp = wload(w_pp, 0, 2, cb * 512, 512)
                for j in range(n):
                    ps, rp = proj_tok(j, wg, rwg, 8, 512, xTk(j), [R("xT", j)])
                    actf(sg[:], ps[:, 0:512], AF.Sigmoid, [rp], [R("rA")])
                    ps, rp = proj_tok(j, wp, rwp, 2, 512, lambda k, j=j: pT[:, k, j * 128:(j + 1) * 128], [R("pT", j)])
                    tt(sg[:], ps[:, 0:512], sg[:], ALU.mult, [rp, R("rA")], [R("rA")])
                    stt(xres[:, j, cb * 512:(cb + 1) * 512], xres[:, j, cb * 512:(cb + 1) * 512], float(ALPHA), sg[:],
                        ALU.mult, ALU.add, [R("rA"), R("xres", j)], [R("xres", j)])
            for j, t in enumerate(tiles):
                layernorm(j, 2, None)
                d = y_dst(t)
                if d is not None:
                    fw.dma(sp, d, xres[:, j, :], reads=[R("xres", j)])

        def kv_dst_prompt(g, t):
            lo = TPQ - GROUPS[g][0] // 128
            if t >= lo:
                return kvo[g][(t - lo) * 128:(t - lo + 1) * 128, :]
            return None

        KV["k"] = kTr; KV["v"] = Vr
        groups = []
        ts_ = list(range(-NHALO, -17))
        for i in range(0, len(ts_), G):
            groups.append((ts_[i:i + G], "state"))
        ts_ = list(range(-17, -1))
        for i in range(0, len(ts_), G):
            groups.append((ts_[i:i + G], "kv"))
        ts_ = list(range(-1, TPQ))
        for i in range(0, len(ts_), G):
            groups.append((ts_[i:i + G], "full"))
        xsrc = lambda t: xh[(t + NHALO) * 128:(t + NHALO + 1) * 128, :]
        ridx = lambda t: t + 17
        firstfull = True
        _ng = int(os.environ.get("K_NGROUPS", "9999"))
        _skip = os.environ.get("K_SKIP", "")
        _glist = groups[:_ng] if "r" not in _skip else groups[-_ng:]

        def issue_xb(tiles_):
            for j_, t_ in enumerate(tiles_):
                fw.dma(pool, xbf[:, j_, :], xsrc(t_), writes=[R("xbf", j_)])

        issue_xb(_glist[0][0])
        for _gi, (tiles, kind) in enumerate(_glist):
            kinds = [kind] * len(tiles)
            if kind != "state":
                conv_issue(999)
            ck(2)
            if kind == "full":
                for j, t in enumerate(tiles):
                    fw.dma(sp, xres[:, j, :], xsrc(t), writes=[R("xres", j)])
            if kind != "state":
                for j, t in enumerate(tiles):
                    fw.dma(sp, ropet[j][:], rope_d[ridx(t)], writes=[R("ropet", j)])
            for j, t in enumerate(tiles):
                transpose_b16(lambda c, j=j: xbf[:, j, c * 128:(c + 1) * 128], 8,
                              lambda c0, n_, j=j: xT[:, c0:c0 + n_, j * 128:(j + 1) * 128], [R("xbf", j)], [R("xT", j)])
            if _gi + 1 < len(_glist):
                issue_xb(_glist[_gi + 1][0])
            ck(3)
            gla_prep(tiles, kinds, False)
            ck(4)
            gla_proj(tiles, kinds, False)
            ck(5)
            gla_seq_prompt(tiles, kinds)
            ck(6)
            if kind == "state" or "a" in _skip:
                conv_issue(2)
                continue
            attn_kv(tiles, kinds, False, ridx, kv_dst_prompt)
            if kind != "full" or "m" in _skip:
                continue
            merge_out(tiles)
            if "f" in _skip:
                continue
            fw.dma(sp, pin[:], ph[(tiles[0] + 1) * 128:(tiles[0] + 2) * 128, :], writes=[R("pin")])
            ffn(tiles, False, firstfull)
            firstfull = False
            ple(tiles, lambda t: ph[(t + 1) * 128:(t + 2) * 128, :],
                lambda t: (y_p[t * 128:(t + 1) * 128, :] if t >= 0 else None), pin_pref=True)
        ck(7)
        fw.dma(sp, gs_p.rearrange("h k v -> k h v"), Sst[:], reads=[R("S")])
        ck(8)
        for f4 in range(6):
            nf = min(4, NFC - f4 * 4)
            ps, rp = pr()
            for c in range(nf):
                fc = f4 * 4 + c
                tp(ps[0:2, c * 128:(c + 1) * 128], tails[:, fc, :], idf[:], [R("tails"), RC], [rp])
            evac(cvo[0:2, 0:nf * 128], ps[0:2, 0:nf * 128], [rp], [R("rA")], eng=dve)
            fw.dma(sp, cv_p[:, f4 * 512:f4 * 512 + nf * 128], cvo[0:2, 0:nf * 128], reads=[R("rA")])

        ck(9)
        fw.barrier()
        ck(10)
        pes.close()
        ses = ExitStack()
        if "s" in _skip:
            fw.finish()
            return nc
        with ses:
            T2 = lambda n, s, d=F32: ses.enter_context(nc.sbuf_tensor("sb_" + n, list(s), d))
            _gb = alloc_group_bufs(lambda n, s, d=F32: T2("s_" + n, s, d), 1)
            xres = _gb["xres"]; xT = _gb["xT"]; glrT = _gb["glrT"]; spb = _gb["spb"]; EbT = _gb["EbT"]; EnbT = _gb["EnbT"]
            Ebl = _gb["Ebl"]; eblT = _gb["eblT"]; qtT = _gb["qtT"]; ktT = _gb["ktT"]; khat = _gb["khat"]; vb = _gb["vb"]
            ggn = _gb["ggn"]; onT = _gb["onT"]; onb = _gb["onb"]; qT = _gb["qT"]; attnT = _gb["attnT"]; mb = _gb["mb"]; mT = _gb["mT"]
            cacc = _gb["cacc"]; gl = _gb["gl"]; yT = _gb["yT"]; pT = _gb["pT"]
            fw.op(dve, lambda: nc.vector.memset(glrT[:], 1.0), writes=[R("glrT")])
            cmask = T2("cmask", (128, 13, 64), BF16); nmask = T2("nmask", (128, 16, 3, 64), BF16)
            bmask = T2("bmask", (128, 16, 128), BF16)
            mtmp = T2("mtmp", (128, 1024))
            fw.dma(sp, mtmp[:, 0:832], cmask_d, writes=[R("mtmp")])
            evac(cmask[:].rearrange("p a b -> p (a b)"), mtmp[:, 0:832], [R("mtmp")], [RC], eng=dve)
            for c3 in range(3):
                fw.dma(sp, mtmp[:, 0:1024], nmask_d[:, c3 * 1024:(c3 + 1) * 1024], writes=[R("mtmp")])
                evac(nmask[:].rearrange("p a b c -> p (a b c)")[:, c3 * 1024:(c3 + 1) * 1024], mtmp[:, 0:1024], [R("mtmp")], [RC], eng=dve)
            for c3 in range(2):
                fw.dma(sp, mtmp[:, 0:1024], bmask_d[:, c3 * 1024:(c3 + 1) * 1024], writes=[R("mtmp")])
                evac(bmask[:].rearrange("p a b -> p (a b)")[:, c3 * 1024:(c3 + 1) * 1024], mtmp[:, 0:1024], [R("mtmp")], [RC], eng=dve)
            qmb = [T2(f"qmb{i}", (128, 4, 128), BF16) for i in range(2)]
            khmb = [T2(f"khmb{i}", (128, 512), BF16) for i in range(2)]
            Sin = [T2(f"Sin{i}", (128, NH, DV)) for i in range(2)]
            Sib = [T2(f"Sib{i}", (128, NH, DV), BF16) for i in range(1)]
            Sout = [T2(f"Sout{i}", (128, NH, DV)) for i in range(1)]
            kvc = [T2(f"kvc{i}", (128, 13, 1024), BF16) for i in range(1)]
            kTc = [T2(f"kTc{i}", (128, 4, 128), BF16) for i in range(3)]
            qbd = [T2(f"qbd{g}", (128, 4, 16, 16), BF16) for g in range(3)]
            vnew = [T2(f"vnew{g}", (128, 512), BF16) for g in range(3)]
            kTn = [T2(f"kTn{g}", (128, 1, 4, 128), BF16) for g in range(3)]
            Vn = [T2(f"Vn{g}", (128, 1, 8, 66), BF16) for g in range(3)]
            PTs = [T2(f"PTs{i}", (128, 16, 64), BF16) for i in range(2)]
            rdens = T2("rdens", (128, 64))
            aexts = T2("aexts", (128, 4, 16, 10))
            ccin = T2("ccin", (32, 512))
            cvs = T2("cvs", (128, NFC, 16, 2))
            KV["k"] = kTn; KV["v"] = Vn

            tiles = [0]; kinds = ["full"]
            load_x(tiles, lambda t: xs_d, lambda t: 33)
            gla_prep(tiles, kinds, True)
            gla_proj(tiles, kinds, True)
            ps, rp = pr()
            for h in range(4):
                mm(ps[:, h * 128:(h + 1) * 128], ktT[:, 0, h, :], qtT[:, 0, h, :], True, True, [R("ktT", 0), R("qtT", 0)], [rp])
            tt(AT[:], ps[:, 0:512].rearrange("p (h t) -> p h t", h=4),
               maskA[:, 1, :].unsqueeze(1).broadcast_to([128, 4, 128]), ALU.mult, [rp, RC], [R("AT")])
            rl = [R("pl", 0), R("pl", 1)]
            for h in range(4):
                o = PL[h // 2][:, (h % 2) * 256:(h % 2 + 1) * 256]
                mm(o, AT[:, h, :], vb[:, 0, h * 256:(h + 1) * 256], h % 2 == 0, False, [R("AT"), R("vb", 0)], [rl[h // 2]])
            def gla_b(b):
                i = b % 2
                tt(qmb[i][:], qtT[:, 0, :, :], bmask[:, b, :].unsqueeze(1).broadcast_to([128, 4, 128]), ALU.mult,
                   [R("qtT", 0), RC], [R("qmb", i)])
                ts(khmb[i][:], khat[:, 0, :], useg[:, 1 + b:2 + b], -16.0, ALU.mult, ALU.mult, [R("khat", 0), RC], [R("khmb", i)])
                evac(Sib[0][:], Sin[i][:], [R("Sin", i)], [R("Sib", 0)], eng=act)
                for h in range(4):
                    o = PL[h // 2][:, (h % 2) * 256:(h % 2 + 1) * 256]
                    mm(o, qmb[i][:, h, :], Sib[0][:, h, :], False, (b == 15 and h % 2 == 1), [R("qmb", i), R("Sib", 0)], [rl[h // 2]])
                for hp in range(2):
                    ps, rp = pr()
                    for e in range(2):
                        h = hp * 2 + e
                        mm(ps[:, e * 256:(e + 1) * 256], khmb[i][:, h * 128:(h + 1) * 128], vb[:, 0, h * 256:(h + 1) * 256],
                           True, True, [R("khmb", i), R("vb", 0)], [rp])
                    for e in range(2):
                        h = hp * 2 + e
                        stt(Sout[0][:, h, :], Sin[i][:, h, :], eblT[:, 0, h * 16 + b:h * 16 + b + 1], ps[:, e * 256:(e + 1) * 256],
                            ALU.mult, ALU.add, [rp, R("eblT", 0), R("Sin", i)], [R("Sout", 0)])
                fw.dma(sp, gs_s[b].rearrange("h k v -> k h v"), Sout[0][:], reads=[R("Sout", 0)])
            for g in range(3):
                fw.op(dve, lambda g=g: nc.vector.memset(Vn[g][:], 1.0), writes=[R("Vr", g, 0)])
            attn_kv(tiles, kinds, True, lambda t: 33, lambda g, t: kvs[g][:, :])
            for g in range(3):
                fw.op(dve, lambda g=g: nc.vector.memset(qbd[g][:], 0.0), writes=[R("qbd", g)])
                for e in range(2):
                    evac(qbd[g][e * 64:(e + 1) * 64, :, :, e * 8:(e + 1) * 8],
                         qT[g][e * 64:(e + 1) * 64, 0, :, :].rearrange("p a (b t) -> p a b t", b=16),
                         [R("qT", g, 0)], [R("qbd", g)], eng=dve)
                evac(vnew[g][:].rearrange("p (h d) -> p h d", h=8), Vn[g][:, 0, :, 0:64], [R("Vr", g, 0)], [R("vnew", g)], eng=dve)
            blk_g = [0] + [1] * 4 + [2] * 8
            segs = [list(range(5, 13)), [0, 1, 2, 3, 4, 13, 14, 15]]
            def attn_b(b):
                kc_ = kvc[0]
                rk12 = R("kvc", "g12"); rk3 = R("kvc", "g3")
                fw.dma(pool, kc_[:, 5:13, :], kvc_d[2][b].rearrange("(a c) e -> a c e", c=16)[:, 0:8, :], writes=[rk3])
                fw.dma(pool, kc_[:, 0, :], kvc_d[0][b], writes=[rk12])
                fw.dma(pool, kc_[:, 1:5, :], kvc_d[1][b].rearrange("(k p) c -> p k c", p=128), writes=[rk12])
                pacc, rpacc = pr()
                first = True
                for si, blks in enumerate(segs):
                    ps, rp = pr()
                    rkc = rk3 if si == 0 else rk12
                    for bi, blk in enumerate(blks):
                        off = bi * 64
                        if blk < 13:
                            g = blk_g[blk]
                            ki = blk % 3
                            transpose_b16(lambda c, blk=blk: kc_[:, blk, c * 128:(c + 1) * 128], 4,
                                          lambda c0, n_, ki=ki: kTc[ki][:, c0:c0 + n_, :], [rkc], [R("kTc", ki)])
                            lk = lambda pair, ki=ki: kTc[ki][:, pair, :]
                            rk = R("kTc", ki)
                        else:
                            g = blk - 13
                            lk = lambda pair, g=g: kTn[g][:, 0, pair, :]
                            rk = R("kTr", g, 0)
                        for pair in range(4):
                            mm(ps[:, off + pair * 16:off + (pair + 1) * 16], lk(pair), qbd[g][:, pair, b, :], True, True,
                               [rk, R("qbd", g)], [rp])
                    P_ = PTs[si]; rP = R("PTs", si)
                    actf(P_[:, 0:8, :].rearrange("p a b -> p (a b)"), ps[:, 0:512], AF.Exp, [rp], [rP], scale=0.125)
                    if si == 0:
                        tt(P_[:, 0:8, :], P_[:, 0:8, :], cmask[:, 5:13, :], ALU.mult, [rP, RC], [rP])
                    else:
                        tt(P_[:, 0:5, :], P_[:, 0:5, :], cmask[:, 0:5, :], ALU.mult, [rP, RC], [rP])
                        tt(P_[:, 5:8, :], P_[:, 5:8, :], nmask[:, b, :, :], ALU.mult, [rP, RC], [rP])
                    for bi, blk in enumerate(blks):
                        mm(pacc[:, 0:64], onesb[:], P_[:, bi, :], first, False, [rP, RC], [rpacc])
                        first = False
                        for pair in range(4):
                            if blk < 13:
                                lv = kc_[:, blk, 512 + pair * 128:512 + (pair + 1) * 128]; rv = rkc
                            else:
                                lv = vnew[blk - 13][:, pair * 128:(pair + 1) * 128]; rv = R("vnew", blk - 13)
                            mm(pacc[:, 64 + pair * 64:64 + (pair + 1) * 64], lv, P_[:, bi, :], False,
                               (si == 1 and bi == 7 and pair == 3), [rP, rv], [rpacc])
                ps, rp = pacc, rpacc
                fw.op(dve, lambda ps=ps: nc.vector.reciprocal(rdens[:], ps[:, 0:64]), reads=[rp], writes=[R("rdens")])
                for e in range(2):
                    pvv = ps[e * 64:(e + 1) * 64, 64:384].rearrange("p (a x) -> p a x", x=80)[:, :, e * 8:e * 8 + 8]
                    rdv = rdens[e * 64:(e + 1) * 64, :].rearrange("p (a x) -> p a x", x=16)[:, :, e * 8:e * 8 + 8]
                    tt(attnT[e * 64:(e + 1) * 64, :, b * 8:(b + 1) * 8], pvv, rdv, ALU.mult, [rp, R("rdens")], [R("attnT", 0)])
            fw.dma(sp, Sin[0][:], st_d[0].rearrange("h k v -> k h v"), writes=[R("Sin", 0)])
            for b in range(16):
                if b + 1 < 16:
                    fw.dma(sp, Sin[(b + 1) % 2][:], st_d[b + 1].rearrange("h k v -> k h v"), writes=[R("Sin", (b + 1) % 2)])
                gla_b(b)
                attn_b(b)
            gla_out_norm(0)
            merge_out(tiles)

            def pre_f4(f4, nf):
                fw.dma(sp, ccin[:, 0:nf * 128], cc_d[:, f4 * 512:f4 * 512 + nf * 128], writes=[R("ccin")])
                ps, rp = pr()
                for c in range(nf):
                    tp(ps[:, c * 32:(c + 1) * 32], ccin[0:32, c * 128:(c + 1) * 128], idf[0:32, 0:32], [R("ccin"), RC], [rp])
                evac(aexts[:, 0:nf, :, 0:2], ps[:, 0:nf * 32].rearrange("p (f b j) -> p f b j", f=nf, b=16),
                     [rp], [R("aexts", c) for c in range(nf)], eng=dve)

            def post_f4(f4, nf):
                evac(cvs[:, f4 * 4:f4 * 4 + nf, :, :], aexts[:, 0:nf, :, 8:10], [R("aexts", c) for c in range(nf)], [R("cvs")], eng=dve)

            ffn(tiles, True, False, aext_s=aexts, pre_f4=pre_f4, post_f4=post_f4)
            for f4 in range(6):
                nf = min(4, NFC - f4 * 4)
                ps, rp = pr()
                for c in range(nf):
                    fc = f4 * 4 + c
                    tp(ps[0:32, c * 128:(c + 1) * 128], cvs[:, fc, :, :].rearrange("p b j -> p (b j)"), idf[:], [R("cvs"), RC], [rp])
                evac(cvo[0:32, 0:nf * 128], ps[0:32, 0:nf * 128], [rp], [R("rA")], eng=dve)
                fw.dma(sp, cv_s[:, f4 * 512:f4 * 512 + nf * 128], cvo[0:32, 0:nf * 128], reads=[R("rA")])
            ple(tiles, lambda t: ps_d, lambda t: y_s)
            fw.finish()
            fw.barrier()
    return nc


def _consts():
    c = {}
    c["idf"] = np.eye(128, dtype=np.float32)
    p = np.arange(128)
    am = np.zeros((8, 128, 128), np.float32)
    kp = p[:, None]; qp = p[None, :]
    am[0] = (kp <= qp); am[1] = (kp >= qp)
    am[2] = (kp <= qp) & ((qp - kp) % 4 == 0); am[3] = ((qp - kp) % 4 == 0); am[4] = (kp >= qp) & ((qp - kp) % 4 == 0)
    am[5] = (kp <= qp) & ((qp - kp) % 16 == 0); am[6] = ((qp - kp) % 16 == 0); am[7] = (kp >= qp) & ((qp - kp) % 16 == 0)
    c["amask"] = am
    um = np.zeros((2, 3, 128, 128), np.float32)
    for k in range(2):
        seg = (p // 8) if k == 1 else np.zeros(128, np.int64)
        same = seg[:, None] == seg[None, :]
        s = p[:, None]; t = p[None, :]
        um[k, 0] = np.where(same & (s <= t), -1.0 / 16, 0.0)
        um[k, 1] = np.where(same & (s > t), -1.0 / 16, 0.0)
    c["um"] = um
    useg = np.zeros((128, 17), np.float32)
    useg[:, 0] = -1.0 / 16
    for b in range(16):
        useg[b * 8:(b + 1) * 8, 1 + b] = -1.0 / 16
    c["useg"] = useg
    cm = np.zeros((128, 13, 8, 8), np.float32)
    t = np.arange(8)[None, :]
    cm[:, 0] = (p[:, None] >= t)[:, None, :]
    for kb in range(4):
        row = 128 * kb + p[:, None]
        cm[:, 1 + kb] = (((row - t) % 4 == 0) & (row >= t))[:, None, :]
    for cc in range(8):
        cm[:, 5 + cc] = (np.full((128, 1), cc) == t)[:, None, :]
    c["cmask"] = cm.reshape(128, 13 * 64)
    nm = np.zeros((128, 16, 3, 8, 8), np.float32)
    bp = p // 8; tp_ = p % 8
    for b in range(16):
        for g, (w, r) in enumerate(GROUPS):
            v = (bp[:, None] == b) & (tp_[:, None] <= t) & ((t - tp_[:, None]) % r == 0)
            nm[:, b, g] = v[:, None, :]
    c["nmask"] = nm.reshape(128, 16 * 3 * 64)
    bm = np.zeros((128, 16, 128), np.float32)
    for b in range(16):
        bm[:, b, b * 8:(b + 1) * 8] = 1.0
    c["bmask"] = bm.reshape(128, 16 * 128)
    return c


def _rope_tables(q):
    half = 32
    inv = (np.float32(10000.0) ** (-np.arange(half, dtype=np.float32) / np.float32(half))).astype(np.float32)
    tabs = np.zeros((34, 128, 96), np.float32)
    for i in range(34):
        if i < 33:
            pos = q * 2048 + (i - 17) * 128 + np.arange(128)
        else:
            pos = SEQ + (np.arange(128) % 8)
        ang = pos.astype(np.float32)[:, None] * inv[None, :]
        tabs[i, :, 0:32] = np.cos(ang); tabs[i, :, 32:64] = np.sin(ang); tabs[i, :, 64:96] = -np.sin(ang)
    return tabs


_NC_CACHE = {}
_PREP_ONLY = [False]


def kernel(x_prompt, x_sample, p_prompt, p_sample, state_gla, cache_conv,
           cache_kv_w128, cache_kv_w512, cache_kv_w2048,
           w_in, w_gk_b, b_gk, gla_norm, w_br_gla, w_br_dil, w_out, ln1_g, ln1_b,
           w_up, conv_w, conv_b, w_down, ln2_g, ln2_b, w_ple_gate, w_ple_proj, ln3_g, ln3_b):
    f = lambda a: np.ascontiguousarray(np.asarray(a, dtype=np.float32))
    x_prompt = f(x_prompt); x_sample = f(x_sample); p_prompt = f(p_prompt); p_sample = f(p_sample)
    state_gla = f(state_gla); cache_conv = f(cache_conv)
    kvcs = [f(cache_kv_w128), f(cache_kv_w512), f(cache_kv_w2048)]
    consts = _consts()
    wgk = np.zeros((32, 512), np.float32)
    wgk[0:16] = f(w_gk_b)[0]; wgk[16] = f(b_gk)[0]
    lnp = np.stack([f(ln1_g)[0], f(ln1_b)[0], f(ln2_g)[0], f(ln2_b)[0], f(ln3_g)[0], f(ln3_b)[0]], 0)
    cwv = np.concatenate([f(conv_w)[0], f(conv_b)], 0)
    cw = np.ascontiguousarray(cwv.reshape(4, NFC, 128).transpose(2, 1, 0)).reshape(128, NFC * 4)
    shared = {
        "w_in": f(w_in)[0], "wgk": wgk, "gnorm": f(gla_norm), "w_brg": f(w_br_gla)[0], "w_brd": f(w_br_dil)[0],
        "w_out": f(w_out)[0], "lnp": lnp, "w_up": f(w_up)[0], "cw": cw, "w_down": f(w_down)[0],
        "w_pg": f(w_ple_gate)[0], "w_pp": f(w_ple_proj)[0],
        "idf": consts["idf"], "amask": consts["amask"], "um": consts["um"], "useg": consts["useg"],
        "cmask": consts["cmask"], "nmask": consts["nmask"], "bmask": consts["bmask"],
    }
    in_maps = []
    for c in range(8):
        b, q = c // 4, c % 4
        xh = np.zeros((64 * 128, D), np.float32)
        lo = q * 2048 - NHALO * 128
        s0 = max(lo, 0)
        xh[s0 - lo:] = x_prompt[b, s0:(q + 1) * 2048]
        ph = np.zeros((17 * 128, 256), np.float32)
        plo = q * 2048 - 128
        p0 = max(plo, 0)
        ph[p0 - plo:] = p_prompt[0, b, p0:(q + 1) * 2048]
        flags = np.zeros((128, 2), np.float32)
        flags[:, 0] = 1.0 if q >= 1 else 0.0
        flags[:, 1] = 1.0 if q >= 2 else 0.0
        m = dict(shared)
        m.update({
            "xh": xh, "ph": ph, "xs": x_sample[c * 16:(c + 1) * 16].reshape(128, D),
            "psm": p_sample[0, c * 16:(c + 1) * 16].reshape(128, 256),
            "st": state_gla[0, c * 16:(c + 1) * 16], "cc": cache_conv[0, c * 16:(c + 1) * 16].reshape(32, DFF),
            "kvc1": kvcs[0][0, c * 16:(c + 1) * 16].reshape(16, 128, 1024),
            "kvc2": kvcs[1][0, c * 16:(c + 1) * 16].reshape(16, 512, 1024),
            "kvc3": kvcs[2][0, c * 16:(c + 1) * 16].reshape(16, 2048, 1024),
            "flags": flags, "rope": _rope_tables(q),
        })
        in_maps.append(m)
    if _PREP_ONLY[0]:
        return in_maps
    if "nc" not in _NC_CACHE:
        _NC_CACHE["nc"] = build_program()
    return _finish(_NC_CACHE["nc"], in_maps)


def _finish(nc, in_maps):
    res = run_bass_kernel_spmd(nc, in_maps, core_ids=list(range(8)))
    return _assemble(res.results)


def _assemble(r):
    y_prompt = np.zeros((2, SEQ, D), np.float32)
    for c in range(8):
        b, q = c // 4, c % 4
        y_prompt[b, q * 2048:(q + 1) * 2048] = r[c]["y_p"]
    y_sample = np.concatenate([r[c]["y_s"].reshape(16, 8, D) for c in range(8)], 0)
    last = [3, 7]
    gla_p = np.stack([r[c]["gs_p"] for c in last], 0)[None]
    gla_s = np.concatenate([r[c]["gs_s"] for c in range(8)], 0)[None]
    conv_p = np.stack([r[c]["cv_p"] for c in last], 0)[None]
    conv_s = np.concatenate([r[c]["cv_s"].reshape(16, 2, DFF) for c in range(8)], 0)[None]
    kvp = []
    for g, (w, _) in enumerate(GROUPS):
        kvp.append(np.stack([r[c][f"kvo{g + 1}"].reshape(w, 2, 8, 64) for c in last], 0)[None])
    kvs_ = []
    for g in range(3):
        kvs_.append(np.concatenate([r[c][f"kvs{g + 1}"].reshape(16, 8, 2, 8, 64) for c in range(8)], 0)[None])
    return (y_prompt, y_sample, gla_p, gla_s, conv_p, conv_s, kvp[0], kvp[1], kvp[2], kvs_[0], kvs_[1], kvs_[2])
```

```python
import os
import numpy as np
from contextlib import ExitStack
import concourse.bass as bass
import concourse.mybir as mybir
from concourse.bass_utils import run_bass_kernel_spmd

F32 = mybir.dt.float32
BF16 = mybir.dt.bfloat16
AF = mybir.ActivationFunctionType
ALU = mybir.AluOpType

D = 1024
SEQ = 8192
NQ = 4
TPQ = 16
NHALO = 48
DK = 128
DV = 256
NH = 4
DFF = 2816
NFC = 22
ALPHA = 2.0 ** 0.25
EPS = 1e-5
GROUPS = ((128, 1), (512, 4), (2048, 16))
NDEL = (2, 5, 17)
C_Q, C_K, C_V, C_G, C_LR = 0, 512, 1024, 2048, 3072
C_DQ, C_DK, C_DV = 3088, 3088 + 1536, 3088 + 3072
C_GA, C_GB = 3088 + 4608, 3088 + 4608 + 1024
INCOLS = 9744


class StopBuild(Exception):
    pass


class Res:
    __slots__ = ("name", "lw", "rd")

    def __init__(self, name):
        self.name = name
        self.lw = None
        self.rd = []


class Eng:
    def __init__(self, name, h, sem):
        self.name = name
        self.h = h
        self.sem = sem
        self.n = 0
        self.seen = {}


class FW:
    def __init__(self, nc, es, n_dma_sems=40):
        self.nc = nc
        mk = lambda n: es.enter_context(nc.semaphore(n))
        self.pe = Eng("pe", nc.tensor, mk("s_pe"))
        self.act = Eng("act", nc.scalar, mk("s_act"))
        self.dve = Eng("dve", nc.vector, mk("s_dve"))
        self.pool = Eng("pool", nc.gpsimd, mk("s_pool"))
        self.sp = Eng("sp", nc.sync, mk("s_sp"))
        self.engs = [self.pe, self.act, self.dve, self.pool, self.sp]
        self.dsems_hw = [[mk(f"s_d{i}"), 0] for i in range(n_dma_sems)]
        self.dsems_sw = [[mk(f"s_w{i}"), 0] for i in range(24)]
        self.dsems = self.dsems_hw + self.dsems_sw
        self.dnext = {"hw": 0, "sw": 0}
        self.res = {}
        self.flip = 0

    def R(self, *key):
        r = self.res.get(key)
        if r is None:
            r = Res(key)
            self.res[key] = r
        return r

    def _deps(self, reads, writes):
        deps = {}

        def add(p):
            if p is None:
                return
            s, v = p
            k = id(s)
            if k not in deps or deps[k][1] < v:
                deps[k] = (s, v)
        for r in reads:
            add(r.lw)
            if r.name[0] in ("pr", "pl", "pb"):
                for p in r.rd:
                    add(p)
        for w in writes:
            add(w.lw)
            for p in w.rd:
                add(p)
        return deps

    def _wait(self, eng, deps, skip_self=False):
        for k, (s, v) in deps.items():
            if skip_self and s is eng.sem:
                continue
            if eng.seen.get(k, 0) >= v:
                continue
            eng.h.wait_ge(s, v)
            eng.seen[k] = v

    def op(self, eng, fn, reads=(), writes=(), skip_self=False):
        deps = self._deps(reads, writes)
        self._wait(eng, deps, skip_self=skip_self)
        ins = fn()
        ins.then_inc(eng.sem, 1)
        eng.n += 1
        tok = (eng.sem, eng.n)
        for w in writes:
            w.lw = tok
            w.rd = []
        for r in reads:
            r.rd.append(tok)
            if len(r.rd) > 16:
                best = {}
                for (s, v) in r.rd:
                    if id(s) not in best or best[id(s)][1] < v:
                        best[id(s)] = (s, v)
                r.rd = list(best.values())
        return ins

    def dma(self, q, out, in_, reads=(), writes=(), **kw):
        deps = self._deps(reads, writes)
        kind = "sw" if q is self.pool else "hw"
        pool_ = self.dsems_sw if kind == "sw" else self.dsems_hw
        slot = pool_[self.dnext[kind]]
        self.dnext[kind] = (self.dnext[kind] + 1) % len(pool_)
        s, v = slot
        if v > 0:
            deps[id(s)] = (s, v)
        self._wait(q, deps)
        ins = q.h.dma_start(out=out, in_=in_, **kw)
        ins.then_inc(s, 16)
        slot[1] = v + 16
        tok = (s, v + 16)
        for w in writes:
            w.lw = tok
            w.rd = []
        for r in reads:
            r.rd.append(tok)
        return ins

    def barrier(self):
        pts = [(e.sem, e.n) for e in self.engs if e.n > 0]
        pts += [(s, v) for (s, v) in self.dsems if v > 0]
        for e in self.engs:
            for (s, v) in pts:
                if s is e.sem or e.seen.get(id(s), 0) >= v:
                    continue
                e.h.wait_ge(s, v)
                e.seen[id(s)] = v

    def finish(self):
        e = self.sp
        pts = [(x.sem, x.n) for x in self.engs if x.n > 0 and x is not e]
        pts += [(s, v) for (s, v) in self.dsems if v > 0]
        for (s, v) in pts:
            if e.seen.get(id(s), 0) >= v:
                continue
            e.h.wait_ge(s, v)
            e.seen[id(s)] = v


def build_program():
    nc = bass.Bass("TRN2", target_bir_lowering=False)
    try:
        return _build(nc)
    except StopBuild:
        return nc


def _build(nc):
    di = lambda n, s: nc.dram_tensor(n, list(s), F32, kind="ExternalInput").ap()
    do = lambda n, s: nc.dram_tensor(n, list(s), F32, kind="ExternalOutput").ap()
    xh = di("xh", (64 * 128, D)); ph = di("ph", (17 * 128, 256))
    xs_d = di("xs", (128, D)); ps_d = di("psm", (128, 256))
    st_d = di("st", (16, NH, DK, DV)); cc_d = di("cc", (32, DFF))
    kvc_d = [di("kvc1", (16, 128, 1024)), di("kvc2", (16, 512, 1024)), di("kvc3", (16, 2048, 1024))]
    w_in = di("w_in", (D, INCOLS)); wgk_d = di("wgk", (32, 512)); gnorm_d = di("gnorm", (1, DV))
    w_brg = di("w_brg", (1024, D)); w_brd = di("w_brd", (512, D)); w_out = di("w_out", (D, D))
    lnp_d = di("lnp", (6, D)); w_up = di("w_up", (D, 2 * DFF)); cw_d = di("cw", (128, NFC * 4))
    w_down = di("w_down", (DFF, D)); w_pg = di("w_pg", (D, D)); w_pp = di("w_pp", (256, D))
    idf_d = di("idf", (128, 128)); amask_d = di("amask", (8, 128, 128)); flags_d = di("flags", (128, 2))
    um_d = di("um", (2, 3, 128, 128)); useg_d = di("useg", (128, 17)); rope_d = di("rope", (34, 128, 96))
    cmask_d = di("cmask", (128, 13 * 64)); nmask_d = di("nmask", (128, 16 * 3 * 64))
    bmask_d = di("bmask", (128, 16 * 128))
    y_p = do("y_p", (TPQ * 128, D)); y_s = do("y_s", (128, D))
    gs_p = do("gs_p", (NH, DK, DV)); gs_s = do("gs_s", (16, NH, DK, DV))
    cv_p = do("cv_p", (2, DFF)); cv_s = do("cv_s", (32, DFF))
    kvo = [do("kvo1", (128, 1024)), do("kvo2", (512, 1024)), do("kvo3", (2048, 1024))]
    kvs = [do("kvs1", (128, 1024)), do("kvs2", (128, 1024)), do("kvs3", (128, 1024))]

    wsrc = {"w_in": (w_in, D, INCOLS), "w_brg": (w_brg, 1024, D), "w_brd": (w_brd, 512, D), "w_out": (w_out, D, D),
            "w_up": (w_up, D, 2 * DFF), "w_down": (w_down, DFF, D), "w_pg": (w_pg, D, D), "w_pp": (w_pp, 256, D)}
    wscr = {k: nc.dram_tensor("scr_" + k, [v[1], v[2]], BF16, kind="Internal").ap() for k, v in wsrc.items()}
    es = ExitStack()
    with es:
        fw = FW(nc, es)
        R = fw.R
        pe, act, dve, pool, sp = fw.pe, fw.act, fw.dve, fw.pool, fw.sp
        _early = int(os.environ.get("K_EARLY", "0"))

        def ck(n):
            if _early == n:
                fw.finish()
                raise StopBuild()
        T = lambda n, s, d=F32: es.enter_context(nc.sbuf_tensor("sb_" + n, list(s), d))
        pes = ExitStack()
        Tp = lambda n, s, d=F32: pes.enter_context(nc.sbuf_tensor("sb_" + n, list(s), d))
        PL = [es.enter_context(nc.psum_tensor(f"pl{i}", [128, 512], F32)) for i in range(2)]
        PR = [es.enter_context(nc.psum_tensor(f"pr{i}", [128, 512], F32)) for i in range(5)]
        PB = [es.enter_context(nc.psum_tensor(f"pb{i}", [128, 1024], BF16)) for i in range(1)]
        rot = {"i": 0, "b": 0}

        def pr():
            i = rot["i"]; rot["i"] = (i + 1) % len(PR)
            return PR[i], R("pr", i)

        def pb():
            return PB[0], R("pb", 0)

        def mm(out, lhsT, rhs, start, stop, reads, writes):
            fw.op(pe, lambda: nc.tensor.matmul(out, lhsT, rhs, start=start, stop=stop),
                  reads=reads, writes=writes, skip_self=True)

        def tp(out, in_, ident, reads, writes):
            fw.op(pe, lambda: nc.tensor.transpose(out, in_, ident), reads=reads, writes=writes, skip_self=True)

        def evac(dst, src, reads, writes, eng=None):
            if eng is None:
                fw.flip ^= 1
                eng = act if fw.flip else dve
            if eng is act:
                fw.op(act, lambda: nc.scalar.copy(dst, src), reads=reads, writes=writes)
            else:
                fw.op(dve, lambda: nc.vector.tensor_copy(dst, src), reads=reads, writes=writes)

        def A(eng_fn_out, *a, **k):
            pass

        def actf(out, in_, func, reads, writes, **kw):
            fw.op(act, lambda: nc.scalar.activation(out, in_, func, **kw), reads=reads, writes=writes)

        def tt(out, in0, in1, op, reads, writes):
            fw.op(dve, lambda: nc.vector.tensor_tensor(out, in0, in1, op=op), reads=reads, writes=writes)

        def ts(out, in0, s1, s2, op0, op1, reads, writes):
            if op1 is None:
                fw.op(dve, lambda: nc.vector.tensor_scalar(out, in0, s1, None, op0=op0), reads=reads, writes=writes)
            else:
                fw.op(dve, lambda: nc.vector.tensor_scalar(out, in0, s1, s2, op0=op0, op1=op1), reads=reads, writes=writes)

        def stt(out, in0, scalar, in1, op0, op1, reads, writes):
            fw.op(dve, lambda: nc.vector.scalar_tensor_tensor(out, in0, scalar, in1, op0=op0, op1=op1),
                  reads=reads, writes=writes)

        idf = T("idf", (128, 128)); idb = T("idb", (128, 128), BF16)
        flags = T("flags", (128, 2))
        um = T("um", (128, 2, 3, 128)); useg = T("useg", (128, 17))
        maskA = T("maskA", (128, 2, 128), BF16)
        wgk = T("wgk", (32, 512)); gnorm = T("gnorm", (128, DV))
        lnp = T("lnp", (128, 2, D)); cw = T("cw", (128, NFC, 4))
        onesb = T("onesb", (128, 128), BF16)
        RC = R("const")
        fw.dma(sp, idf[:], idf_d, writes=[RC])
        fw.dma(sp, flags[:], flags_d, writes=[RC])
        fw.dma(sp, um[:], um_d.rearrange("k j s t -> s k j t"), writes=[RC])
        fw.dma(sp, useg[:], useg_d, writes=[RC])
        fw.dma(sp, wgk[:], wgk_d, writes=[RC])
        fw.dma(sp, gnorm[:], gnorm_d.partition_broadcast(128), writes=[RC])
        fw.dma(sp, cw[:], cw_d.rearrange("p (f j) -> p f j", j=4), writes=[RC])
        evac(idb[:], idf[:], [RC], [RC], eng=dve)
        for k in range(2):
            ts(maskA[:, k, :], um[:, k, 0, :], 0.0, None, ALU.not_equal, None, [RC], [RC])
        fw.op(dve, lambda: nc.vector.memset(onesb[:], 1.0), writes=[RC])

        conv_pending = []
        for k_ in ("w_in", "w_up", "w_down", "w_brg", "w_brd", "w_out", "w_pg", "w_pp"):
            src_, rows_, cols_ = wsrc[k_]
            r0 = 0
            while r0 < rows_:
                rn = min(1024, rows_ - r0)
                c0 = 0
                while c0 < cols_:
                    cn = min(2048, cols_ - c0)
                    conv_pending.append((k_, r0, rn, c0, cn))
                    c0 += cn
                r0 += rn

        def conv_issue(n_):
            for _ in range(n_):
                if not conv_pending:
                    return
                k_, r0, rn, c0, cn = conv_pending.pop(0)
                fw.dma(pool, wscr[k_][r0:r0 + rn, c0:c0 + cn], wsrc[k_][0][r0:r0 + rn, c0:c0 + cn],
                       writes=[R("wscr", k_, r0, c0)])

        conv_issue(2)
        ck(1)
        G = int(os.environ.get("K_G", "2"))
        NMAX = G * 128
        NWB = 4
        wbuf = [T(f"wb{i}", (128, 8, 512), BF16) for i in range(NWB)]
        wst = {"i": 0}
        gtmp = T("gtmp", (128, 512))
        AT = T("AT", (128, 4, 128), BF16)
        ssq = T("ssq", (128, 8))
        junk = T("junk", (128, 256)) if os.environ.get("K_T1") else gtmp
        ropet = [T(f"ropet{i}", (128, 96)) for i in range(2)]
        rst = {"i": 0}
        rA = T("rA", (128, 512)); rB = T("rB", (128, 512))
        sg = rA; t1 = rB
        if os.environ.get("K_T2"):
            sg = T("sg", (128, 512)); t1 = T("t1", (128, 512))
        kvout = [T(f"kvout{i}", (128, 2, 512)) for i in range(1)]
        kvst = {"i": 0}
        qr = T("qr", (128, 512))
        PT = [T(f"PT{i}", (128, 4, 128), BF16) for i in range(4)]
        ptst = {"i": 0}
        rden = T("rden", (128, 64))
        attnb = T("attnb", (128, 512), BF16)
        stats = T("stats", (128, 2, 6)); mv = T("mv", (128, 2)); rs = T("rs", (128, 2))
        aest = {"i": 0}
        pin = T("pin", (128, 256))
        cvo = rA

        def alloc_group_bufs(TT, g_):
            n_ = g_ * 128
            d = {}
            d["xres"] = TT("xres", (128, g_, D)); d["xT"] = TT("xT", (128, 8, n_), BF16)
            d["glrT"] = TT("glrT", (32, n_))
            d["spb"] = TT("spb", (128, g_, 512)); d["EbT"] = TT("EbT", (128, g_, 512), BF16)
            d["EnbT"] = TT("EnbT", (128, g_, 512), BF16)
            d["Ebl"] = TT("Ebl", (128, g_, 512), BF16); d["eblT"] = TT("eblT", (128, g_, 64))
            d["qtT"] = TT("qtT", (128, g_, 4, 128), BF16); d["ktT"] = TT("ktT", (128, g_, 4, 128), BF16)
            d["khat"] = TT("khat", (128, g_, 512), BF16); d["vb"] = TT("vb", (128, g_, 1024), BF16)
            d["ggn"] = TT("ggn", (128, g_, 1024), BF16)
            d["onT"] = TT("onT", (128, 8, n_), BF16); d["onb"] = TT("onb", (128, g_, 1024), BF16)
            d["qT"] = [TT(f"qT{g}", (128, g_, 4, 128), BF16) for g in range(3)]
            d["attnT"] = TT("attnT", (128, 4, n_), BF16)
            d["mb"] = d["ggn"]; d["mT"] = d["onT"]
            d["cacc"] = TT("cacc", (128, n_)); d["gl"] = TT("gl", (128, n_))
            d["yT"] = TT("yT", (128, NFC, n_), BF16); d["pT"] = TT("pT", (128, 2, n_), BF16)
            return d

        _gb = alloc_group_bufs(lambda n, s, d=F32: Tp("g_" + n, s, d), G)
        xres = _gb["xres"]; xT = _gb["xT"]; glrT = _gb["glrT"]; spb = _gb["spb"]; EbT = _gb["EbT"]; EnbT = _gb["EnbT"]
        Ebl = _gb["Ebl"]; eblT = _gb["eblT"]; qtT = _gb["qtT"]; ktT = _gb["ktT"]; khat = _gb["khat"]; vb = _gb["vb"]
        ggn = _gb["ggn"]; onT = _gb["onT"]; onb = _gb["onb"]; qT = _gb["qT"]; attnT = _gb["attnT"]; mb = _gb["mb"]; mT = _gb["mT"]
        cacc = _gb["cacc"]; gl = _gb["gl"]; yT = _gb["yT"]; pT = _gb["pT"]
        amask = Tp("amask", (128, 8, 128), BF16); amaskp = Tp("amaskp", (128, 8, 128), BF16)
        amaskq = Tp("amaskq", (128, 128), BF16)
        Sst = Tp("Sst", (128, NH, DV)); Sbf = Tp("Sbf", (128, NH, DV), BF16)
        xbf = Tp("xbf", (128, G, D), BF16)
        aext = [Tp(f"aext{i}", (128, 2 + NMAX)) for i in range(2)]
        tails = Tp("tails", (128, NFC, 2))
        _pad = int(os.environ.get("K_PAD", "0"))
        if _pad:
            padbuf = Tp("padbuf", (128, _pad))
        RS = [NDEL[g] + G - 1 for g in range(3)]
        kTr = [Tp(f"kTr{g}", (128, RS[g], 4, 128), BF16) for g in range(3)]
        Vr = [Tp(f"Vr{g}", (128, RS[g], 8, 66), BF16) for g in range(3)]
        for m4 in range(2):
            ctmp = gtmp[:].rearrange("p (m t) -> p m t", m=4)
            fw.dma(sp, ctmp, amask_d[m4 * 4:(m4 + 1) * 4].rearrange("m s t -> s m t"), writes=[R("gtmp")])
            evac(amask[:, m4 * 4:(m4 + 1) * 4, :], ctmp, [R("gtmp")], [RC], eng=dve)
            ts(amaskp[:, m4 * 4:(m4 + 1) * 4, :], ctmp, flags[:, 0:1], None, ALU.mult, None, [R("gtmp"), RC], [RC])
            if m4 == 1:
                ts(amaskq[:], ctmp[:, 3, :], flags[:, 1:2], None, ALU.mult, None, [R("gtmp"), RC], [RC])
        for g in range(3):
            fw.op(dve, lambda g=g: nc.vector.memset(Vr[g][:], 1.0), writes=[R("Vr", g, s_) for s_ in range(RS[g])])

        fw.op(dve, lambda: nc.vector.memset(glrT[:], 1.0), writes=[R("glrT")])
        fw.op(dve, lambda: nc.vector.memset(Sst[:], 0.0), writes=[R("S")])
        fw.op(dve, lambda: nc.vector.memset(Sbf[:], 0.0), writes=[R("Sbf")])
        fw.op(dve, lambda: nc.vector.memset(tails[:], 0.0), writes=[R("tails")])

        wname = {id(v[0]): k for k, v in wsrc.items()}

        def wload(w_ap, k0, nk, col0, cw_):
            i = wst["i"]; wst["i"] = (i + 1) % NWB
            buf = wbuf[i]
            k_ = wname[id(w_ap)]
            rows_, cols_ = wsrc[k_][1], wsrc[k_][2]
            src = wscr[k_][k0 * 128:(k0 + nk) * 128, col0:col0 + cw_].rearrange("(k p) c -> p k c", p=128)
            deps = []
            r0 = 0
            while r0 < rows_:
                c0 = 0
                while c0 < cols_:
                    if r0 < (k0 + nk) * 128 and r0 + 1024 > k0 * 128 and c0 < col0 + cw_ and c0 + 2048 > col0:
                        deps.append(R("wscr", k_, r0, c0))
                    c0 += 2048
                r0 += 1024
            fw.dma(pool, buf[:, 0:nk, 0:cw_], src, reads=deps, writes=[R("wb", i)])
            return buf, R("wb", i)

        def issue_x(tiles, src_fn, rope_idx=None):
            for j, t in enumerate(tiles):
                fw.dma(sp, xres[:, j, :], src_fn(t), writes=[R("xres", j)])

        def load_x(tiles, src_fn, rope_idx=None, issued=False):
            if not issued:
                issue_x(tiles, src_fn, rope_idx)
            if rope_idx is not None:
                for j, t in enumerate(tiles):
                    fw.dma(sp, ropet[j][:], rope_d[rope_idx(t)], writes=[R("ropet", j)])
            for j, t in enumerate(tiles):
                transpose_f32(lambda c, j=j: xres[:, j, c * 128:(c + 1) * 128], 8,
                              lambda c0, n, j=j: xT[:, c0:c0 + n, j * 128:(j + 1) * 128],
                              [R("xres", j)], [R("xT", j)])

        def transpose_f32(src_fn, nch, dst_fn, reads, writes):
            c = 0
            while c < nch:
                n = min(4, nch - c)
                ps, rp = pr()
                for i in range(n):
                    tp(ps[:, i * 128:(i + 1) * 128], src_fn(c + i), idf[:], reads + [RC], [rp])
                evac(dst_fn(c, n), ps[:, 0:n * 128].rearrange("p (a b) -> p a b", a=n), [rp], writes)
                c += n

        def transpose_b16(src_fn, nch, dst_fn, reads, writes):
            ps, rp = pb()
            for i in range(nch):
                tp(ps[:, i * 128:(i + 1) * 128], src_fn(i), idb[:], reads + [RC], [rp])
            evac(dst_fn(0, nch), ps[:, 0:nch * 128].rearrange("p (a b) -> p a b", a=nch), [rp], writes)

        def proj_tok(j, wl, rw, nk, cw_, kfn, reads):
            ps, rp = pr()
            for k in range(nk):
                mm(ps[:, 0:cw_], kfn(k), wl[:, k, 0:cw_], k == 0, k == nk - 1, reads + [rw], [rp])
            return ps, rp

        xTk = lambda j: (lambda k: xT[:, k, j * 128:(j + 1) * 128])

        def gla_prep(tiles, kinds, sample):
            n = len(tiles); N = n * 128
            ku = 1 if sample else 0
            nseg = 16 if sample else 1
            rx = [R("xT", j) for j in range(n)]
            anyfull = any(k == "full" for k in kinds)
            wl, rw = wload(w_in, 0, 8, C_LR, 16)
            ps, rp = pr()
            for k in range(8):
                mm(ps[0:16, 0:N], wl[:, k, 0:16], xT[:, k, 0:N], k == 0, k == 7, rx + [rw], [rp])
            evac(glrT[0:16, 0:N], ps[0:16, 0:N], [rp], [R("glrT")])
            for cb in range(2):
                wl, rw = wload(w_in, 0, 8, C_V + cb * 512, 512)
                for j in range(n):
                    ps, rp = proj_tok(j, wl, rw, 8, 512, xTk(j), [R("xT", j)])
                    evac(vb[:, j, cb * 512:(cb + 1) * 512], ps[:, 0:512], [rp], [R("vb", j)])
            for j in range(n):
                ps, rp = pr()
                mm(ps[:, 0:512], glrT[0:32, j * 128:(j + 1) * 128], wgk[0:32, :], True, True, [R("glrT"), RC], [rp])
                actf(spb[:, j, :], ps[:, 0:512], AF.Exp, [rp], [R("spb", j)], scale=-1.0)
                actf(spb[:, j, :], spb[:, j, :], AF.Ln, [R("spb", j)], [R("spb", j)], bias=1.0)
            if anyfull:
                for cb in range(2):
                    wl, rw = wload(w_in, 0, 8, C_G + cb * 512, 512)
                    for j in range(n):
                        if kinds[j] != "full":
                            continue
                        ps, rp = proj_tok(j, wl, rw, 8, 512, xTk(j), [R("xT", j)])
                        actf(gtmp[:, 0:512], ps[:, 0:512], AF.Silu, [rp], [R("gtmp")])
                        tt(ggn[:, j, cb * 512:(cb + 1) * 512].rearrange("p (h v) -> p h v", h=2),
                           gtmp[:, 0:512].rearrange("p (h v) -> p h v", h=2),
                           gnorm[:].unsqueeze(1).broadcast_to([128, 2, DV]), ALU.mult,
                           [R("gtmp"), RC], [R("ggn", j)])
            for j in range(n):
                if kinds[j] == "full":
                    ps, rp = pr()
                    for h in range(4):
                        mm(ps[:, h * 128:(h + 1) * 128], spb[:, j, h * 128:(h + 1) * 128], um[:, ku, 0, :], True, True,
                           [R("spb", j), RC], [rp])
                    actf(EbT[:, j, :], ps[:, 0:512], AF.Exp, [rp], [R("EbT", j)])
                    actf(EnbT[:, j, :], ps[:, 0:512], AF.Exp, [rp], [R("EnbT", j)], scale=-1.0)
                ps, rp = pr()
                mm(ps[:, 0:512], um[:, ku, 1, :], spb[:, j, :], True, True, [R("spb", j), RC], [rp])
                actf(Ebl[:, j, :], ps[:, 0:512], AF.Exp, [rp], [R("Ebl", j)])
                ps, rp = pr()
                for h in range(4):
                    rhs = useg[:, 1:17] if sample else useg[:, 0:1]
                    mm(ps[:, h * 16:h * 16 + nseg], spb[:, j, h * 128:(h + 1) * 128], rhs, True, True,
                       [R("spb", j), RC], [rp])
                if sample:
                    actf(eblT[:, j, :], ps[:, 0:64], AF.Exp, [rp], [R("eblT", j)])
                else:
                    for h in range(4):
                        actf(eblT[:, j, h * 16:h * 16 + 1], ps[:, h * 16:h * 16 + 1], AF.Exp, [rp], [R("eblT", j)])
            wl, rw = wload(w_in, 0, 8, C_K, 512)
            if anyfull:
                for h in range(4):
                    ps, rp = pr()
                    for k in range(8):
                        mm(ps[:, 0:N], wl[:, k, h * 128:(h + 1) * 128], xT[:, k, 0:N], k == 0, k == 7, rx + [rw], [rp])
                    for j in range(n):
                        if kinds[j] == "full":
                            tt(ktT[:, j, h, :], ps[:, j * 128:(j + 1) * 128], EnbT[:, j, h * 128:(h + 1) * 128], ALU.mult,
                               [rp, R("EnbT", j)], [R("ktT", j)])
            for j in range(n):
                ps, rp = proj_tok(j, wl, rw, 8, 512, xTk(j), [R("xT", j)])
                tt(khat[:, j, :], ps[:, 0:512], Ebl[:, j, :], ALU.mult, [rp, R("Ebl", j)], [R("khat", j)])
            if anyfull:
                wl, rw = wload(w_in, 0, 8, C_Q, 512)
                for h in range(4):
                    ps, rp = pr()
                    for k in range(8):
                        mm(ps[:, 0:N], wl[:, k, h * 128:(h + 1) * 128], xT[:, k, 0:N], k == 0, k == 7, rx + [rw], [rp])
                    for j in range(n):
                        if kinds[j] == "full":
                            stt(qtT[:, j, h, :], ps[:, j * 128:(j + 1) * 128], float(DK ** -0.5),
                                EbT[:, j, h * 128:(h + 1) * 128], ALU.mult, ALU.mult,
                                [rp, R("EbT", j)], [R("qtT", j)])

        def gla_proj(tiles, kinds, sample):
            return

        def gla_out_norm(j):
            rl = [R("pl", 0), R("pl", 1)]
            for h in range(4):
                fw.op(act, lambda h=h: nc.scalar.activation(junk[:, 0:256], PL[h // 2][:, (h % 2) * 256:(h % 2 + 1) * 256],
                                                            AF.Square, accum_out=ssq[:, h:h + 1]),
                      reads=[rl[h // 2]], writes=[R("gtmp"), R("ssq")])
            actf(ssq[:, 4:8], ssq[:, 0:4], AF.Ln, [R("ssq")], [R("ssq")], scale=1.0 / DV, bias=EPS)
            actf(ssq[:, 4:8], ssq[:, 4:8], AF.Exp, [R("ssq")], [R("ssq")], scale=-0.5)
            for h in range(4):
                stt(onb[:, j, h * 256:(h + 1) * 256], PL[h // 2][:, (h % 2) * 256:(h % 2 + 1) * 256], ssq[:, 4 + h:5 + h],
                    ggn[:, j, h * 256:(h + 1) * 256], ALU.mult, ALU.mult,
                    [rl[h // 2], R("ssq"), R("ggn", j)], [R("onb", j)])

        def gla_out_T(j):
            transpose_b16(lambda c: onb[:, j, c * 128:(c + 1) * 128], 8,
                          lambda c0, n_: onT[:, c0:c0 + n_, j * 128:(j + 1) * 128], [R("onb", j)], [R("onT", j)])

        def gla_seq_prompt(tiles, kinds):
            for j, t in enumerate(tiles):
                if kinds[j] == "full":
                    ps, rp = pr()
                    for h in range(4):
                        mm(ps[:, h * 128:(h + 1) * 128], ktT[:, j, h, :], qtT[:, j, h, :], True, True,
                           [R("ktT", j), R("qtT", j)], [rp])
                    tt(AT[:].rearrange("p h t -> p h t"), ps[:, 0:512].rearrange("p (h t) -> p h t", h=4),
                       maskA[:, 0, :].unsqueeze(1).broadcast_to([128, 4, 128]), ALU.mult, [rp, RC], [R("AT")])
                    for h in range(4):
                        o = PL[h // 2][:, (h % 2) * 256:(h % 2 + 1) * 256]
                        rl = R("pl", h // 2)
                        mm(o, AT[:, h, :], vb[:, j, h * 256:(h + 1) * 256], h % 2 == 0, False, [R("AT"), R("vb", j)], [rl])
                        mm(o, qtT[:, j, h, :], Sbf[:, h, :], False, h % 2 == 1, [R("qtT", j), R("Sbf")], [rl])
                for hp in range(2):
                    ps, rp = pr()
                    for e in range(2):
                        h = hp * 2 + e
                        mm(ps[:, e * 256:(e + 1) * 256], khat[:, j, h * 128:(h + 1) * 128], vb[:, j, h * 256:(h + 1) * 256],
                           True, True, [R("khat", j), R("vb", j)], [rp])
                    for e in range(2):
                        h = hp * 2 + e
                        stt(Sst[:, h, :], Sst[:, h, :], eblT[:, j, h * 16:h * 16 + 1], ps[:, e * 256:(e + 1) * 256],
                            ALU.mult, ALU.add, [rp, R("eblT", j), R("S")], [R("S")])
                evac(Sbf[:], Sst[:], [R("S")], [R("Sbf")], eng=act)
                if kinds[j] == "full":
                    gla_out_norm(j)

        def rope_to(dst, ps, rp, rt, rrt, writes):
            X = ps[:, 0:512].rearrange("p (h a d) -> p h a d", h=8, a=2)
            cosb = rt[:, 0:32].unsqueeze(1).unsqueeze(1).broadcast_to([128, 8, 2, 32])
            sinb = rt[:, 32:64].unsqueeze(1).broadcast_to([128, 8, 32])
            nsinb = rt[:, 64:96].unsqueeze(1).broadcast_to([128, 8, 32])
            tt(rA[:].rearrange("p (h a d) -> p h a d", h=8, a=2), X, cosb, ALU.mult, [rp, rrt], [R("rA")])
            Bv = rB[:].rearrange("p (h a d) -> p h a d", h=8, a=2)
            tt(Bv[:, :, 0, :], X[:, :, 1, :], nsinb, ALU.mult, [rp, rrt], [R("rB")])
            tt(Bv[:, :, 1, :], X[:, :, 0, :], sinb, ALU.mult, [rp, rrt], [R("rB")])
            tt(dst, rA[:], rB[:], ALU.add, [R("rA"), R("rB")], writes)

        KV = {}

        def attn_kv(tiles, kinds, sample, rope_idx, kv_dst):
            n = len(tiles)
            for g in range(3):
                need = [kinds[j] == "full" or (kinds[j] == "kv" and tiles[j] >= -1 - GROUPS[g][0] // 128) for j in range(n)]
                if not any(need):
                    continue
                wk, rwk = wload(w_in, 0, 8, C_DK + g * 512, 512)
                wv, rwv = wload(w_in, 0, 8, C_DV + g * 512, 512)
                anyfull = any(k == "full" for k in kinds)
                if anyfull:
                    wq, rwq = wload(w_in, 0, 8, C_DQ + g * 512, 512)
                for j, t in enumerate(tiles):
                    if not need[j]:
                        continue
                    rt = ropet[j]; rrt = R("ropet", j)
                    ko = 0
                    kvt = kvout[ko]; rkv = R("kvout", ko)
                    slot = 0 if sample else (t % RS[g])
                    kTr_ = KV["k"]; Vr_ = KV["v"]
                    psk, rpk = proj_tok(j, wk, rwk, 8, 512, xTk(j), [R("xT", j)])
                    isfull = kinds[j] == "full"
                    if isfull:
                        psq, rpq = proj_tok(j, wq, rwq, 8, 512, xTk(j), [R("xT", j)])
                    psv, rpv = proj_tok(j, wv, rwv, 8, 512, xTk(j), [R("xT", j)])
                    rope_to(kvt[:, 0, :], psk, rpk, rt, rrt, [rkv])
                    evac(kvt[:, 1, :], psv[:, 0:512], [rpv], [rkv], eng=act)
                    if isfull:
                        rope_to(qr[:], psq, rpq, rt, rrt, [R("qr")])
                    evac(Vr_[g][:, slot, :, 0:64], psv[:, 0:512].rearrange("p (h d) -> p h d", h=8), [rpv],
                         [R("Vr", g, slot)], eng=dve)
                    dst = kv_dst(g, t)
                    if dst is not None:
                        fw.dma(sp, dst, kvt[:].rearrange("p a c -> p (a c)"), reads=[rkv])
                    transpose_f32(lambda c: kvt[:, 0, c * 128:(c + 1) * 128], 4,
                                  lambda c0, n_: kTr_[g][:, slot, c0:c0 + n_, :], [rkv], [R("kTr", g, slot)])
                    if isfull:
                        transpose_f32(lambda c: qr[:, c * 128:(c + 1) * 128], 4,
                                      lambda c0, n_, j=j: qT[g][:, j, c0:c0 + n_, :], [R("qr")], [R("qT", g, j)])
                        if g == 2 and not sample:
                            attn_prompt(j, t)

        def attn_prompt(j, t):
            first = [True, True]
            rl = [R("pl", 0), R("pl", 1)]
            steps = []
            for g in range(3):
                for dl in range(NDEL[g]):
                    kt = t - dl
                    slot = kt % RS[g]
                    if g == 0:
                        mi = dl
                    elif g == 1:
                        mi = 2 + (0 if dl == 0 else (2 if dl == 4 else 1))
                    else:
                        mi = 5 + (0 if dl == 0 else (2 if dl == 16 else 1))
                    if kt == t or kt >= 0:
                        mk = amask[:, mi, :]
                    elif kt == -17:
                        mk = amaskq[:]
                    else:
                        mk = amaskp[:, mi, :]
                    for half in range(2):
                        steps.append((g, dl, slot, mk, half))

            def scores(st):
                g, dl, slot, mk, half = st
                e = half
                ps, rp = pr()
                for hh in range(4):
                    mm(ps[:, hh * 128:(hh + 1) * 128], kTr[g][e * 64:(e + 1) * 64, slot, hh, :],
                       qT[g][e * 64:(e + 1) * 64, j, hh, :], True, True, [R("kTr", g, slot), R("qT", g, j)], [rp])
                pi = ptst["i"]; ptst["i"] = (pi + 1) % len(PT)
                P_ = PT[pi]; rP = R("PT", pi)
                actf(P_[:].rearrange("p h t -> p (h t)"), ps[:, 0:512], AF.Exp, [rp], [rP], scale=0.125)
                tt(P_[:], P_[:], mk.unsqueeze(1).broadcast_to([128, 4, 128]), ALU.mult, [rP, RC], [rP])
                return P_, rP

            def pvmm(st, P_, rP, last):
                g, dl, slot, mk, half = st
                for hh in range(4):
                    h = 2 * hh + half
                    mm(PL[half][:, hh * 65:(hh + 1) * 65], P_[:, hh, :], Vr[g][:, slot, h, 0:65], first[half],
                       (last and hh == 3), [rP, R("Vr", g, slot)], [rl[half]])
                    first[half] = False

            pend = []
            ns_ = len(steps)
            for i_, st in enumerate(steps):
                P_, rP = scores(st)
                pend.append((i_, st, P_, rP))
                if len(pend) > 2:
                    i0, st0, P0, rP0 = pend.pop(0)
                    pvmm(st0, P0, rP0, i0 >= ns_ - 2)
            while pend:
                i0, st0, P0, rP0 = pend.pop(0)
                pvmm(st0, P0, rP0, i0 >= ns_ - 2)
            for half in range(2):
                pv = PL[half][:, 0:260].rearrange("p (h d) -> p h d", h=4)
                fw.op(dve, lambda half=half, pv=pv: nc.vector.reciprocal(rden[:, half * 4:half * 4 + 4], pv[:, :, 64]),
                      reads=[rl[half]], writes=[R("rden")])
                tt(attnb[:].rearrange("p (a e d) -> p a e d", a=4, e=2)[:, :, half, :], pv[:, :, 0:64],
                   rden[:, half * 4:half * 4 + 4].unsqueeze(2).broadcast_to([128, 4, 64]), ALU.mult,
                   [rl[half], R("rden")], [R("attnb")])
            transpose_b16(lambda c: attnb[:, c * 128:(c + 1) * 128], 4,
                          lambda c0, n_: attnT[:, c0:c0 + n_, j * 128:(j + 1) * 128], [R("attnb")], [R("attnT", j)])

        def ln_load(li):
            for i_ in range(2):
                fw.dma(sp, lnp[:, i_, :], lnp_d[2 * li + i_:2 * li + i_ + 1, :].partition_broadcast(128), writes=[R("lnp")])

        def layernorm(j, li, src_writes):
            rx = R("xres", j)
            for c in range(2):
                fw.op(dve, lambda c=c: nc.vector.bn_stats(stats[:, c, :], xres[:, j, c * 512:(c + 1) * 512]),
                      reads=[rx], writes=[R("stats")])
            fw.op(dve, lambda: nc.vector.bn_aggr(mv[:], stats[:].rearrange("p a b -> p (a b)")), reads=[R("stats")], writes=[R("mv")])
            actf(rs[:, 0:1], mv[:, 1:2], AF.Ln, [R("mv")], [R("rs")], bias=EPS)
            actf(rs[:, 0:1], rs[:, 0:1], AF.Exp, [R("rs")], [R("rs")], scale=-0.5)
            ts(xres[:, j, :], xres[:, j, :], mv[:, 0:1], rs[:, 0:1], ALU.subtract, ALU.mult, [rx, R("mv"), R("rs")], [rx])
            tt(xres[:, j, :], xres[:, j, :], lnp[:, 0, :], ALU.mult, [rx, R("lnp")], [rx])
            tt(xres[:, j, :], xres[:, j, :], lnp[:, 1, :], ALU.add, [rx, R("lnp")], [rx])

        def merge_out(tiles):
            n = len(tiles)
            ln_load(0)
            for j in range(n):
                gla_out_T(j)
            for cb in range(2):
                wga, rga = wload(w_in, 0, 8, C_GA + cb * 512, 512)
                wya, rya = wload(w_brg, 0, 8, cb * 512, 512)
                wgb, rgb = wload(w_in, 0, 8, C_GB + cb * 512, 512)
                wyb, ryb = wload(w_brd, 0, 4, cb * 512, 512)
                for j in range(n):
                    ps, rp = proj_tok(j, wga, rga, 8, 512, xTk(j), [R("xT", j)])
                    actf(sg[:], ps[:, 0:512], AF.Sigmoid, [rp], [R("rA")])
                    ps, rp = proj_tok(j, wya, rya, 8, 512, lambda k, j=j: onT[:, k, j * 128:(j + 1) * 128], [R("onT", j)])
                    tt(t1[:], ps[:, 0:512], sg[:], ALU.mult, [rp, R("rA")], [R("rB")])
                    ps, rp = proj_tok(j, wgb, rgb, 8, 512, xTk(j), [R("xT", j)])
                    actf(sg[:], ps[:, 0:512], AF.Sigmoid, [rp], [R("rA")])
                    ps, rp = proj_tok(j, wyb, ryb, 4, 512, lambda k, j=j: attnT[:, k, j * 128:(j + 1) * 128], [R("attnT", j)])
                    tt(sg[:], ps[:, 0:512], sg[:], ALU.mult, [rp, R("rA")], [R("rA")])
                    tt(mb[:, j, cb * 512:(cb + 1) * 512], sg[:], t1[:], ALU.add, [R("rA"), R("rB")], [R("ggn", j)])
            for j in range(n):
                transpose_b16(lambda c, j=j: mb[:, j, c * 128:(c + 1) * 128], 8,
                              lambda c0, n_, j=j: mT[:, c0:c0 + n_, j * 128:(j + 1) * 128], [R("ggn", j)], [R("onT", j)])
            for cb in range(2):
                wl, rw = wload(w_out, 0, 8, cb * 512, 512)
                for j in range(n):
                    ps, rp = proj_tok(j, wl, rw, 8, 512, lambda k, j=j: mT[:, k, j * 128:(j + 1) * 128], [R("onT", j)])
                    stt(xres[:, j, cb * 512:(cb + 1) * 512], xres[:, j, cb * 512:(cb + 1) * 512], float(ALPHA), ps[:, 0:512],
                        ALU.mult, ALU.add, [rp, R("xres", j)], [R("xres", j)])
            for j in range(n):
                layernorm(j, 0, None)
                transpose_f32(lambda c, j=j: xres[:, j, c * 128:(c + 1) * 128], 8,
                              lambda c0, n_, j=j: xT[:, c0:c0 + n_, j * 128:(j + 1) * 128], [R("xres", j)], [R("xT", j)])

        def ffn(tiles, sample, first_group, aext_s=None, pre_f4=None, post_f4=None):
            n = len(tiles); N = n * 128
            rx = [R("xT", j) for j in range(n)]
            ln_load(1)
            for f4 in range(6):
                nf = min(4, NFC - f4 * 4)
                wa, rwa = wload(w_up, 0, 8, f4 * 512, nf * 128)
                wu, rwu = wload(w_up, 0, 8, DFF + f4 * 512, nf * 128)
                if pre_f4 is not None:
                    pre_f4(f4, nf)
                for c in range(nf):
                    fc = f4 * 4 + c
                    ps, rp = pr()
                    for k in range(8):
                        mm(ps[:, 0:N], wa[:, k, c * 128:(c + 1) * 128], xT[:, k, 0:N], k == 0, k == 7, rx + [rwa], [rp])
                    if sample:
                        ae = aext_s[:, c, :, :]
                        rae = R("aexts", c)
                        evac(ae[:, :, 2:10], ps[:, 0:128].rearrange("p (b t) -> p b t", b=16), [rp], [rae], eng=act)
                        sh = lambda o: ae[:, :, o:o + 8]
                        cv = cacc[:, 0:128].rearrange("p (b t) -> p b t", b=16)
                    else:
                        ai = aest["i"]; aest["i"] ^= 1
                        ae = aext[ai]; rae = R("aext", ai)
                        evac(ae[:, 0:2], tails[:, fc, :], [R("tails")], [rae], eng=act)
                        evac(ae[:, 2:2 + N], ps[:, 0:N], [rp], [rae], eng=act)
                        if first_group:
                            ts(ae[:, 2:130], ae[:, 2:130], flags[:, 0:1], None, ALU.mult, None, [rae, RC], [rae])
                        evac(tails[:, fc, :], ae[:, N:N + 2], [rae], [R("tails")], eng=act)
                        sh = lambda o: ae[:, o:o + N]
                        cv = cacc[:, 0:N]
                    ts(cv, sh(0), cw[:, fc, 0:1], cw[:, fc, 3:4], ALU.mult, ALU.add, [rae, RC], [R("cacc")])
                    stt(cv, sh(1), cw[:, fc, 1:2], cv, ALU.mult, ALU.add, [rae, RC, R("cacc")], [R("cacc")])
                    stt(cv, sh(2), cw[:, fc, 2:3], cv, ALU.mult, ALU.add, [rae, RC, R("cacc")], [R("cacc")])
                    actf(gl[:, 0:N], cacc[:, 0:N], AF.Gelu, [R("cacc")], [R("gl")])
                    ps, rp = pr()
                    for k in range(8):
                        mm(ps[:, 0:N], wu[:, k, c * 128:(c + 1) * 128], xT[:, k, 0:N], k == 0, k == 7, rx + [rwu], [rp])
                    tt(yT[:, fc, 0:N], ps[:, 0:N], gl[:, 0:N], ALU.mult, [rp, R("gl")], [R("yT")])
                if post_f4 is not None:
                    post_f4(f4, nf)
            for cb in range(2):
                accs = [pr() for _ in range(n)]
                for fg in range(3):
                    k0 = fg * 8; nk = min(8, NFC - k0)
                    wl, rw = wload(w_down, k0, nk, cb * 512, 512)
                    for j in range(n):
                        ps, rp = accs[j]
                        for k in range(nk):
                            fc = k0 + k
                            mm(ps[:, 0:512], yT[:, fc, j * 128:(j + 1) * 128], wl[:, k, :], fc == 0, fc == NFC - 1,
                               [R("yT"), rw], [rp])
                for j in range(n):
                    ps, rp = accs[j]
                    stt(xres[:, j, cb * 512:(cb + 1) * 512], xres[:, j, cb * 512:(cb + 1) * 512], float(ALPHA), ps[:, 0:512],
                        ALU.mult, ALU.add, [rp, R("xres", j)], [R("xres", j)])
            for j in range(n):
                layernorm(j, 1, None)
                transpose_f32(lambda c, j=j: xres[:, j, c * 128:(c + 1) * 128], 8,
                              lambda c0, n_, j=j: xT[:, c0:c0 + n_, j * 128:(j + 1) * 128], [R("xres", j)], [R("xT", j)])

        def ple(tiles, p_src, y_dst, pin_pref=False):
            n = len(tiles)
            ln_load(2)
            for j, t in enumerate(tiles):
                if not (j == 0 and pin_pref):
                    fw.dma(sp, pin[:], p_src(t), writes=[R("pin")])
                transpose_f32(lambda c: pin[:, c * 128:(c + 1) * 128], 2,
                              lambda c0, n_, j=j: pT[:, c0:c0 + n_, j * 128:(j + 1) * 128], [R("pin")], [R("pT", j)])
            for cb in range(2):
                wg, rwg = wload(w_pg, 0, 8, cb * 512, 512)
                wp, rwp = wload(w_pp, 0, 2, cb * 512, 512)
                for j in range(n):
                    ps, rp = proj_tok(j, wg, rwg, 8, 512, xTk(j), [R("xT", j)])
                    actf(sg[:], ps[:, 0:512], AF.Sigmoid, [rp], [R("rA")])
                    ps, rp = proj_tok(j, wp, rwp, 2, 512, lambda k, j=j: pT[:, k, j * 128:(j + 1) * 128], [R("pT", j)])
                    tt(sg[:], ps[:, 0:512], sg[:], ALU.mult, [rp, R("rA")], [R("rA")])
                    stt(xres[:, j, cb * 512:(cb + 1) * 512], xres[:, j, cb * 512:(cb + 1) * 512], float(ALPHA), sg[:],
                        ALU.mult, ALU.add, [R("rA"), R("xres", j)], [R("xres", j)])
            for j, t in enumerate(tiles):
                layernorm(j, 2, None)
                d = y_dst(t)
                if d is not None:
                    fw.dma(sp, d, xres[:, j, :], reads=[R("xres", j)])

        def kv_dst_prompt(g, t):
            lo = TPQ - GROUPS[g][0] // 128
            if t >= lo:
                return kvo[g][(t - lo) * 128:(t - lo + 1) * 128, :]
            return None

        KV["k"] = kTr; KV["v"] = Vr
        groups = []
        ts_ = list(range(-NHALO, -17))
        for i in range(0, len(ts_), G):
            groups.append((ts_[i:i + G], "state"))
        ts_ = list(range(-17, -1))
        for i in range(0, len(ts_), G):
            groups.append((ts_[i:i + G], "kv"))
        ts_ = list(range(-1, TPQ))
        for i in range(0, len(ts_), G):
            groups.append((ts_[i:i + G], "full"))
        xsrc = lambda t: xh[(t + NHALO) * 128:(t + NHALO + 1) * 128, :]
        ridx = lambda t: t + 17
        firstfull = True
        _ng = int(os.environ.get("K_NGROUPS", "9999"))
        _skip = os.environ.get("K_SKIP", "")
        _glist = groups[:_ng] if "r" not in _skip else groups[-_ng:]

        def issue_xb(tiles_):
            for j_, t_ in enumerate(tiles_):
                fw.dma(pool, xbf[:, j_, :], xsrc(t_), writes=[R("xbf", j_)])

        _issued = False
        if _glist[0][1] == "full":
            issue_xb(_glist[0][0])
        for _gi, (tiles, kind) in enumerate(_glist):
            kinds = [kind] * len(tiles)
            if kind != "state":
                conv_issue(999)
            ck(2)
            if kind == "full":
                for j, t in enumerate(tiles):
                    fw.dma(sp, xres[:, j, :], xsrc(t), writes=[R("xres", j)])
                for j, t in enumerate(tiles):
                    fw.dma(sp, ropet[j][:], rope_d[ridx(t)], writes=[R("ropet", j)])
                for j, t in enumerate(tiles):
                    transpose_b16(lambda c, j=j: xbf[:, j, c * 128:(c + 1) * 128], 8,
                                  lambda c0, n_, j=j: xT[:, c0:c0 + n_, j * 128:(j + 1) * 128], [R("xbf", j)], [R("xT", j)])
            else:
                load_x(tiles, xsrc, ridx if kind != "state" else None, issued=_issued)
                _issued = False
            if _gi + 1 < len(_glist):
                _nt, _nk = _glist[_gi + 1]
                if _nk == "full":
                    issue_xb(_nt)
                elif kind != "full":
                    issue_x(_nt, xsrc)
                    _issued = True
            ck(3)
            gla_prep(tiles, kinds, False)
            ck(4)
            gla_proj(tiles, kinds, False)
            ck(5)
            gla_seq_prompt(tiles, kinds)
            ck(6)
            if kind == "state" or "a" in _skip:
                conv_issue(2)
                continue
            attn_kv(tiles, kinds, False, ridx, kv_dst_prompt)
            if kind != "full" or "m" in _skip:
                continue
            merge_out(tiles)
            if "f" in _skip:
                continue
            fw.dma(sp, pin[:], ph[(tiles[0] + 1) * 128:(tiles[0] + 2) * 128, :], writes=[R("pin")])
            ffn(tiles, False, firstfull)
            firstfull = False
            ple(tiles, lambda t: ph[(t + 1) * 128:(t + 2) * 128, :],
                lambda t: (y_p[t * 128:(t + 1) * 128, :] if t >= 0 else None), pin_pref=True)
        ck(7)
        fw.dma(sp, gs_p.rearrange("h k v -> k h v"), Sst[:], reads=[R("S")])
        ck(8)
        for f4 in range(6):
            nf = min(4, NFC - f4 * 4)
            ps, rp = pr()
            for c in range(nf):
                fc = f4 * 4 + c
                tp(ps[0:2, c * 128:(c + 1) * 128], tails[:, fc, :], idf[:], [R("tails"), RC], [rp])
            evac(cvo[0:2, 0:nf * 128], ps[0:2, 0:nf * 128], [rp], [R("rA")], eng=dve)
            fw.dma(sp, cv_p[:, f4 * 512:f4 * 512 + nf * 128], cvo[0:2, 0:nf * 128], reads=[R("rA")])

        ck(9)
        fw.barrier()
        ck(10)
        pes.close()
        ses = ExitStack()
        if "s" in _skip:
            fw.finish()
            return nc
        with ses:
            T2 = lambda n, s, d=F32: ses.enter_context(nc.sbuf_tensor("sb_" + n, list(s), d))
            _gb = alloc_group_bufs(lambda n, s, d=F32: T2("s_" + n, s, d), 1)
            xres = _gb["xres"]; xT = _gb["xT"]; glrT = _gb["glrT"]; spb = _gb["spb"]; EbT = _gb["EbT"]; EnbT = _gb["EnbT"]
            Ebl = _gb["Ebl"]; eblT = _gb["eblT"]; qtT = _gb["qtT"]; ktT = _gb["ktT"]; khat = _gb["khat"]; vb = _gb["vb"]
            ggn = _gb["ggn"]; onT = _gb["onT"]; onb = _gb["onb"]; qT = _gb["qT"]; attnT = _gb["attnT"]; mb = _gb["mb"]; mT = _gb["mT"]
            cacc = _gb["cacc"]; gl = _gb["gl"]; yT = _gb["yT"]; pT = _gb["pT"]
            fw.op(dve, lambda: nc.vector.memset(glrT[:], 1.0), writes=[R("glrT")])
            cmask = T2("cmask", (128, 13, 64), BF16); nmask = T2("nmask", (128, 16, 3, 64), BF16)
            bmask = T2("bmask", (128, 16, 128), BF16)
            mtmp = T2("mtmp", (128, 1024))
            fw.dma(sp, mtmp[:, 0:832], cmask_d, writes=[R("mtmp")])
            evac(cmask[:].rearrange("p a b -> p (a b)"), mtmp[:, 0:832], [R("mtmp")], [RC], eng=dve)
            for c3 in range(3):
                fw.dma(sp, mtmp[:, 0:1024], nmask_d[:, c3 * 1024:(c3 + 1) * 1024], writes=[R("mtmp")])
                evac(nmask[:].rearrange("p a b c -> p (a b c)")[:, c3 * 1024:(c3 + 1) * 1024], mtmp[:, 0:1024], [R("mtmp")], [RC], eng=dve)
            for c3 in range(2):
                fw.dma(sp, mtmp[:, 0:1024], bmask_d[:, c3 * 1024:(c3 + 1) * 1024], writes=[R("mtmp")])
                evac(bmask[:].rearrange("p a b -> p (a b)")[:, c3 * 1024:(c3 + 1) * 1024], mtmp[:, 0:1024], [R("mtmp")], [RC], eng=dve)
            qmb = [T2(f"qmb{i}", (128, 4, 128), BF16) for i in range(2)]
            khmb = [T2(f"khmb{i}", (128, 512), BF16) for i in range(2)]
            Sin = [T2(f"Sin{i}", (128, NH, DV)) for i in range(2)]
            Sib = [T2(f"Sib{i}", (128, NH, DV), BF16) for i in range(1)]
            Sout = [T2(f"Sout{i}", (128, NH, DV)) for i in range(1)]
            kvc = [T2(f"kvc{i}", (128, 13, 1024), BF16) for i in range(1)]
            kTc = [T2(f"kTc{i}", (128, 4, 128), BF16) for i in range(3)]
            qbd = [T2(f"qbd{g}", (128, 4, 16, 16), BF16) for g in range(3)]
            vnew = [T2(f"vnew{g}", (128, 512), BF16) for g in range(3)]
            kTn = [T2(f"kTn{g}", (128, 1, 4, 128), BF16) for g in range(3)]
            Vn = [T2(f"Vn{g}", (128, 1, 8, 66), BF16) for g in range(3)]
            PTs = [T2(f"PTs{i}", (128, 16, 64), BF16) for i in range(2)]
            rdens = T2("rdens", (128, 64))
            aexts = T2("aexts", (128, 4, 16, 10))
            ccin = T2("ccin", (32, 512))
            cvs = T2("cvs", (128, NFC, 16, 2))
            KV["k"] = kTn; KV["v"] = Vn

            tiles = [0]; kinds = ["full"]
            load_x(tiles, lambda t: xs_d, lambda t: 33)
            gla_prep(tiles, kinds, True)
            gla_proj(tiles, kinds, True)
            ps, rp = pr()
            for h in range(4):
                mm(ps[:, h * 128:(h + 1) * 128], ktT[:, 0, h, :], qtT[:, 0, h, :], True, True, [R("ktT", 0), R("qtT", 0)], [rp])
            tt(AT[:], ps[:, 0:512].rearrange("p (h t) -> p h t", h=4),
               maskA[:, 1, :].unsqueeze(1).broadcast_to([128, 4, 128]), ALU.mult, [rp, RC], [R("AT")])
            rl = [R("pl", 0), R("pl", 1)]
            for h in range(4):
                o = PL[h // 2][:, (h % 2) * 256:(h % 2 + 1) * 256]
                mm(o, AT[:, h, :], vb[:, 0, h * 256:(h + 1) * 256], h % 2 == 0, False, [R("AT"), R("vb", 0)], [rl[h // 2]])
            def gla_b(b):
                i = b % 2
                tt(qmb[i][:], qtT[:, 0, :, :], bmask[:, b, :].unsqueeze(1).broadcast_to([128, 4, 128]), ALU.mult,
                   [R("qtT", 0), RC], [R("qmb", i)])
                ts(khmb[i][:], khat[:, 0, :], useg[:, 1 + b:2 + b], -16.0, ALU.mult, ALU.mult, [R("khat", 0), RC], [R("khmb", i)])
                evac(Sib[0][:], Sin[i][:], [R("Sin", i)], [R("Sib", 0)], eng=act)
                for h in range(4):
                    o = PL[h // 2][:, (h % 2) * 256:(h % 2 + 1) * 256]
                    mm(o, qmb[i][:, h, :], Sib[0][:, h, :], False, (b == 15 and h % 2 == 1), [R("qmb", i), R("Sib", 0)], [rl[h // 2]])
                for hp in range(2):
                    ps, rp = pr()
                    for e in range(2):
                        h = hp * 2 + e
                        mm(ps[:, e * 256:(e + 1) * 256], khmb[i][:, h * 128:(h + 1) * 128], vb[:, 0, h * 256:(h + 1) * 256],
                           True, True, [R("khmb", i), R("vb", 0)], [rp])
                    for e in range(2):
                        h = hp * 2 + e
                        stt(Sout[0][:, h, :], Sin[i][:, h, :], eblT[:, 0, h * 16 + b:h * 16 + b + 1], ps[:, e * 256:(e + 1) * 256],
                            ALU.mult, ALU.add, [rp, R("eblT", 0), R("Sin", i)], [R("Sout", 0)])
                fw.dma(sp, gs_s[b].rearrange("h k v -> k h v"), Sout[0][:], reads=[R("Sout", 0)])
            for g in range(3):
                fw.op(dve, lambda g=g: nc.vector.memset(Vn[g][:], 1.0), writes=[R("Vr", g, 0)])
            attn_kv(tiles, kinds, True, lambda t: 33, lambda g, t: kvs[g][:, :])
            for g in range(3):
                fw.op(dve, lambda g=g: nc.vector.memset(qbd[g][:], 0.0), writes=[R("qbd", g)])
                for e in range(2):
                    evac(qbd[g][e * 64:(e + 1) * 64, :, :, e * 8:(e + 1) * 8],
                         qT[g][e * 64:(e + 1) * 64, 0, :, :].rearrange("p a (b t) -> p a b t", b=16),
                         [R("qT", g, 0)], [R("qbd", g)], eng=dve)
                evac(vnew[g][:].rearrange("p (h d) -> p h d", h=8), Vn[g][:, 0, :, 0:64], [R("Vr", g, 0)], [R("vnew", g)], eng=dve)
            blk_g = [0] + [1] * 4 + [2] * 8
            segs = [list(range(5, 13)), [0, 1, 2, 3, 4, 13, 14, 15]]
            def attn_b(b):
                kc_ = kvc[0]
                rk12 = R("kvc", "g12"); rk3 = R("kvc", "g3")
                fw.dma(pool, kc_[:, 5:13, :], kvc_d[2][b].rearrange("(a c) e -> a c e", c=16)[:, 0:8, :], writes=[rk3])
                fw.dma(pool, kc_[:, 0, :], kvc_d[0][b], writes=[rk12])
                fw.dma(pool, kc_[:, 1:5, :], kvc_d[1][b].rearrange("(k p) c -> p k c", p=128), writes=[rk12])
                pacc, rpacc = pr()
                first = True
                for si, blks in enumerate(segs):
                    ps, rp = pr()
                    rkc = rk3 if si == 0 else rk12
                    for bi, blk in enumerate(blks):
                        off = bi * 64
                        if blk < 13:
                            g = blk_g[blk]
                            ki = blk % 3
                            transpose_b16(lambda c, blk=blk: kc_[:, blk, c * 128:(c + 1) * 128], 4,
                                          lambda c0, n_, ki=ki: kTc[ki][:, c0:c0 + n_, :], [rkc], [R("kTc", ki)])
                            lk = lambda pair, ki=ki: kTc[ki][:, pair, :]
                            rk = R("kTc", ki)
                        else:
                            g = blk - 13
                            lk = lambda pair, g=g: kTn[g][:, 0, pair, :]
                            rk = R("kTr", g, 0)
                        for pair in range(4):
                            mm(ps[:, off + pair * 16:off + (pair + 1) * 16], lk(pair), qbd[g][:, pair, b, :], True, True,
                               [rk, R("qbd", g)], [rp])
                    P_ = PTs[si]; rP = R("PTs", si)
                    actf(P_[:, 0:8, :].rearrange("p a b -> p (a b)"), ps[:, 0:512], AF.Exp, [rp], [rP], scale=0.125)
                    if si == 0:
                        tt(P_[:, 0:8, :], P_[:, 0:8, :], cmask[:, 5:13, :], ALU.mult, [rP, RC], [rP])
                    else:
                        tt(P_[:, 0:5, :], P_[:, 0:5, :], cmask[:, 0:5, :], ALU.mult, [rP, RC], [rP])
                        tt(P_[:, 5:8, :], P_[:, 5:8, :], nmask[:, b, :, :], ALU.mult, [rP, RC], [rP])
                    for bi, blk in enumerate(blks):
                        mm(pacc[:, 0:64], onesb[:], P_[:, bi, :], first, False, [rP, RC], [rpacc])
                        first = False
                        for pair in range(4):
                            if blk < 13:
                                lv = kc_[:, blk, 512 + pair * 128:512 + (pair + 1) * 128]; rv = rkc
                            else:
                                lv = vnew[blk - 13][:, pair * 128:(pair + 1) * 128]; rv = R("vnew", blk - 13)
                            mm(pacc[:, 64 + pair * 64:64 + (pair + 1) * 64], lv, P_[:, bi, :], False,
                               (si == 1 and bi == 7 and pair == 3), [rP, rv], [rpacc])
                ps, rp = pacc, rpacc
                fw.op(dve, lambda ps=ps: nc.vector.reciprocal(rdens[:], ps[:, 0:64]), reads=[rp], writes=[R("rdens")])
                for e in range(2):
                    pvv = ps[e * 64:(e + 1) * 64, 64:384].rearrange("p (a x) -> p a x", x=80)[:, :, e * 8:e * 8 + 8]
                    rdv = rdens[e * 64:(e + 1) * 64, :].rearrange("p (a x) -> p a x", x=16)[:, :, e * 8:e * 8 + 8]
                    tt(attnT[e * 64:(e + 1) * 64, :, b * 8:(b + 1) * 8], pvv, rdv, ALU.mult, [rp, R("rdens")], [R("attnT", 0)])
            fw.dma(sp, Sin[0][:], st_d[0].rearrange("h k v -> k h v"), writes=[R("Sin", 0)])
            for b in range(16):
                if b + 1 < 16:
                    fw.dma(sp, Sin[(b + 1) % 2][:], st_d[b + 1].rearrange("h k v -> k h v"), writes=[R("Sin", (b + 1) % 2)])
                gla_b(b)
                attn_b(b)
            gla_out_norm(0)
            merge_out(tiles)

            def pre_f4(f4, nf):
                fw.dma(sp, ccin[:, 0:nf * 128], cc_d[:, f4 * 512:f4 * 512 + nf * 128], writes=[R("ccin")])
                ps, rp = pr()
                for c in range(nf):
                    tp(ps[:, c * 32:(c + 1) * 32], ccin[0:32, c * 128:(c + 1) * 128], idf[0:32, 0:32], [R("ccin"), RC], [rp])
                evac(aexts[:, 0:nf, :, 0:2], ps[:, 0:nf * 32].rearrange("p (f b j) -> p f b j", f=nf, b=16),
                     [rp], [R("aexts", c) for c in range(nf)], eng=dve)

            def post_f4(f4, nf):
                evac(cvs[:, f4 * 4:f4 * 4 + nf, :, :], aexts[:, 0:nf, :, 8:10], [R("aexts", c) for c in range(nf)], [R("cvs")], eng=dve)

            ffn(tiles, True, False, aext_s=aexts, pre_f4=pre_f4, post_f4=post_f4)
            for f4 in range(6):
                nf = min(4, NFC - f4 * 4)
                ps, rp = pr()
                for c in range(nf):
                    fc = f4 * 4 + c
                    tp(ps[0:32, c * 128:(c + 1) * 128], cvs[:, fc, :, :].rearrange("p b j -> p (b j)"), idf[:], [R("cvs"), RC], [rp])
                evac(cvo[0:32, 0:nf * 128], ps[0:32, 0:nf * 128], [rp], [R("rA")], eng=dve)
                fw.dma(sp, cv_s[:, f4 * 512:f4 * 512 + nf * 128], cvo[0:32, 0:nf * 128], reads=[R("rA")])
            ple(tiles, lambda t: ps_d, lambda t: y_s)
            fw.finish()
            fw.barrier()
    return nc


def _consts():
    c = {}
    c["idf"] = np.eye(128, dtype=np.float32)
    p = np.arange(128)
    am = np.zeros((8, 128, 128), np.float32)
    kp = p[:, None]; qp = p[None, :]
    am[0] = (kp <= qp); am[1] = (kp >= qp)
    am[2] = (kp <= qp) & ((qp - kp) % 4 == 0); am[3] = ((qp - kp) % 4 == 0); am[4] = (kp >= qp) & ((qp - kp) % 4 == 0)
    am[5] = (kp <= qp) & ((qp - kp) % 16 == 0); am[6] = ((qp - kp) % 16 == 0); am[7] = (kp >= qp) & ((qp - kp) % 16 == 0)
    c["amask"] = am
    um = np.zeros((2, 3, 128, 128), np.float32)
    for k in range(2):
        seg = (p // 8) if k == 1 else np.zeros(128, np.int64)
        same = seg[:, None] == seg[None, :]
        s = p[:, None]; t = p[None, :]
        um[k, 0] = np.where(same & (s <= t), -1.0 / 16, 0.0)
        um[k, 1] = np.where(same & (s > t), -1.0 / 16, 0.0)
    c["um"] = um
    useg = np.zeros((128, 17), np.float32)
    useg[:, 0] = -1.0 / 16
    for b in range(16):
        useg[b * 8:(b + 1) * 8, 1 + b] = -1.0 / 16
    c["useg"] = useg
    cm = np.zeros((128, 13, 8, 8), np.float32)
    t = np.arange(8)[None, :]
    cm[:, 0] = (p[:, None] >= t)[:, None, :]
    for kb in range(4):
        row = 128 * kb + p[:, None]
        cm[:, 1 + kb] = (((row - t) % 4 == 0) & (row >= t))[:, None, :]
    for cc in range(8):
        cm[:, 5 + cc] = (np.full((128, 1), cc) == t)[:, None, :]
    c["cmask"] = cm.reshape(128, 13 * 64)
    nm = np.zeros((128, 16, 3, 8, 8), np.float32)
    bp = p // 8; tp_ = p % 8
    for b in range(16):
        for g, (w, r) in enumerate(GROUPS):
            v = (bp[:, None] == b) & (tp_[:, None] <= t) & ((t - tp_[:, None]) % r == 0)
            nm[:, b, g] = v[:, None, :]
    c["nmask"] = nm.reshape(128, 16 * 3 * 64)
    bm = np.zeros((128, 16, 128), np.float32)
    for b in range(16):
        bm[:, b, b * 8:(b + 1) * 8] = 1.0
    c["bmask"] = bm.reshape(128, 16 * 128)
    return c


def _rope_tables(q):
    half = 32
    inv = (np.float32(10000.0) ** (-np.arange(half, dtype=np.float32) / np.float32(half))).astype(np.float32)
    tabs = np.zeros((34, 128, 96), np.float32)
    for i in range(34):
        if i < 33:
            pos = q * 2048 + (i - 17) * 128 + np.arange(128)
        else:
            pos = SEQ + (np.arange(128) % 8)
        ang = pos.astype(np.float32)[:, None] * inv[None, :]
        tabs[i, :, 0:32] = np.cos(ang); tabs[i, :, 32:64] = np.sin(ang); tabs[i, :, 64:96] = -np.sin(ang)
    return tabs


_NC_CACHE = {}
_PREP_ONLY = [False]


def kernel(x_prompt, x_sample, p_prompt, p_sample, state_gla, cache_conv,
           cache_kv_w128, cache_kv_w512, cache_kv_w2048,
           w_in, w_gk_b, b_gk, gla_norm, w_br_gla, w_br_dil, w_out, ln1_g, ln1_b,
           w_up, conv_w, conv_b, w_down, ln2_g, ln2_b, w_ple_gate, w_ple_proj, ln3_g, ln3_b):
    f = lambda a: np.ascontiguousarray(np.asarray(a, dtype=np.float32))
    x_prompt = f(x_prompt); x_sample = f(x_sample); p_prompt = f(p_prompt); p_sample = f(p_sample)
    state_gla = f(state_gla); cache_conv = f(cache_conv)
    kvcs = [f(cache_kv_w128), f(cache_kv_w512), f(cache_kv_w2048)]
    consts = _consts()
    wgk = np.zeros((32, 512), np.float32)
    wgk[0:16] = f(w_gk_b)[0]; wgk[16] = f(b_gk)[0]
    lnp = np.stack([f(ln1_g)[0], f(ln1_b)[0], f(ln2_g)[0], f(ln2_b)[0], f(ln3_g)[0], f(ln3_b)[0]], 0)
    cwv = np.concatenate([f(conv_w)[0], f(conv_b)], 0)
    cw = np.ascontiguousarray(cwv.reshape(4, NFC, 128).transpose(2, 1, 0)).reshape(128, NFC * 4)
    shared = {
        "w_in": f(w_in)[0], "wgk": wgk, "gnorm": f(gla_norm), "w_brg": f(w_br_gla)[0], "w_brd": f(w_br_dil)[0],
        "w_out": f(w_out)[0], "lnp": lnp, "w_up": f(w_up)[0], "cw": cw, "w_down": f(w_down)[0],
        "w_pg": f(w_ple_gate)[0], "w_pp": f(w_ple_proj)[0],
        "idf": consts["idf"], "amask": consts["amask"], "um": consts["um"], "useg": consts["useg"],
        "cmask": consts["cmask"], "nmask": consts["nmask"], "bmask": consts["bmask"],
    }
    in_maps = []
    for c in range(8):
        b, q = c // 4, c % 4
        xh = np.zeros((64 * 128, D), np.float32)
        lo = q * 2048 - NHALO * 128
        s0 = max(lo, 0)
        xh[s0 - lo:] = x_prompt[b, s0:(q + 1) * 2048]
        ph = np.zeros((17 * 128, 256), np.float32)
        plo = q * 2048 - 128
        p0 = max(plo, 0)
        ph[p0 - plo:] = p_prompt[0, b, p0:(q + 1) * 2048]
        flags = np.zeros((128, 2), np.float32)
        flags[:, 0] = 1.0 if q >= 1 else 0.0
        flags[:, 1] = 1.0 if q >= 2 else 0.0
        m = dict(shared)
        m.update({
            "xh": xh, "ph": ph, "xs": x_sample[c * 16:(c + 1) * 16].reshape(128, D),
            "psm": p_sample[0, c * 16:(c + 1) * 16].reshape(128, 256),
            "st": state_gla[0, c * 16:(c + 1) * 16], "cc": cache_conv[0, c * 16:(c + 1) * 16].reshape(32, DFF),
            "kvc1": kvcs[0][0, c * 16:(c + 1) * 16].reshape(16, 128, 1024),
            "kvc2": kvcs[1][0, c * 16:(c + 1) * 16].reshape(16, 512, 1024),
            "kvc3": kvcs[2][0, c * 16:(c + 1) * 16].reshape(16, 2048, 1024),
            "flags": flags, "rope": _rope_tables(q),
        })
        in_maps.append(m)
    if _PREP_ONLY[0]:
        return in_maps
    if "nc" not in _NC_CACHE:
        _NC_CACHE["nc"] = build_program()
    return _finish(_NC_CACHE["nc"], in_maps)


def _finish(nc, in_maps):
    res = run_bass_kernel_spmd(nc, in_maps, core_ids=list(range(8)))
    return _assemble(res.results)


def _assemble(r):
    y_prompt = np.zeros((2, SEQ, D), np.float32)
    for c in range(8):
        b, q = c // 4, c % 4
        y_prompt[b, q * 2048:(q + 1) * 2048] = r[c]["y_p"]
    y_sample = np.concatenate([r[c]["y_s"].reshape(16, 8, D) for c in range(8)], 0)
    last = [3, 7]
    gla_p = np.stack([r[c]["gs_p"] for c in last], 0)[None]
    gla_s = np.concatenate([r[c]["gs_s"] for c in range(8)], 0)[None]
    conv_p = np.stack([r[c]["cv_p"] for c in last], 0)[None]
    conv_s = np.concatenate([r[c]["cv_s"].reshape(16, 2, DFF) for c in range(8)], 0)[None]
    kvp = []
    for g, (w, _) in enumerate(GROUPS):
        kvp.append(np.stack([r[c][f"kvo{g + 1}"].reshape(w, 2, 8, 64) for c in last], 0)[None])
    kvs_ = []
    for g in range(3):
        kvs_.append(np.concatenate([r[c][f"kvs{g + 1}"].reshape(16, 8, 2, 8, 64) for c in range(8)], 0)[None])
    return (y_prompt, y_sample, gla_p, gla_s, conv_p, conv_s, kvp[0], kvp[1], kvp[2], kvs_[0], kvs_[1], kvs_[2])
```
